# Optimizing a Trainium2 kernel written in Bass

```python
import math
import jax, jax.numpy as jnp
from jax import lax
import numpy as np

D_MODEL = 1024
BATCH = 16
SEQ = 2048
DEPTH = 2

CHUNK = 64
D_MIX = D_MODEL
N_MIXERS = 4
GROUP_W = D_MIX // N_MIXERS
POOL_CH = GROUP_W
POOL_WINDOWS = (2, 4, 8, 16)
POOL_GROUPS = len(POOL_WINDOWS)
POOL_GROUP_CH = POOL_CH // POOL_GROUPS
SGU_CH = GROUP_W
SGU_HEADS = 4
SGU_HEAD_CH = SGU_CH // SGU_HEADS
SGU_BLOCK = 128
SCONV_CH = GROUP_W
SCONV_WIDTH = 3
SSD_INNER = GROUP_W
SSD_HEADDIM = 64
SSD_HEADS = SSD_INNER // SSD_HEADDIM
SSD_GROUPS = 2
SSD_STATE = 128
SSD_CONV = 4
SSD_CHUNK = CHUNK
SSD_BC = SSD_GROUPS * SSD_STATE
SSD_CONV_CH = SSD_INNER + 2 * SSD_BC
IN_COLS = POOL_CH + 2 * SGU_CH + 3 * SCONV_CH + 2 * SSD_INNER + 2 * SSD_BC + SSD_HEADS
D_FF = ((8 * D_MODEL + 3 * 256 - 1) // (3 * 256)) * 256
EPS = 1e-6

kernel_name = 'hybrid_parallel_head_group_streaming_encoder'


def _rmsnorm(x, g):
    xf = x.astype(jnp.float32)
    y = xf * lax.rsqrt(jnp.mean(jnp.square(xf), axis=-1, keepdims=True) + EPS)
    return (y * g.astype(jnp.float32)).astype(x.dtype)


def _split_columns(proj):
    sizes = [POOL_CH, SGU_CH, SGU_CH, SCONV_CH, SCONV_CH, SCONV_CH,
             SSD_INNER, SSD_INNER, SSD_BC, SSD_BC, SSD_HEADS]
    idx = [int(v) for v in np.cumsum(sizes)[:-1]]
    return jnp.split(proj, idx, axis=-1)


def _causal_dwconv(x, w):
    k_taps = w.shape[0]
    s = x.shape[1]
    xp = jnp.pad(x, ((0, 0), (k_taps - 1, 0), (0, 0)))
    out = xp[:, 0:s] * w[0]
    for k in range(1, k_taps):
        out = out + xp[:, k:k + s] * w[k]
    return out


def _pool_mixer(xp, w, bias, scale):
    b, s, _ = xp.shape
    xg = xp.reshape(b, s, POOL_GROUPS, POOL_GROUP_CH)
    xf = xg.astype(jnp.float32)
    cs = jnp.cumsum(xf, axis=1)
    pos = jnp.arange(s, dtype=jnp.float32)
    means = []
    for g, win in enumerate(POOL_WINDOWS):
        c = cs[:, :, g]
        lagged = jnp.pad(c, ((0, 0), (win, 0), (0, 0)))[:, :s]
        count = jnp.minimum(pos + 1.0, float(win))[:, None]
        means.append((c - lagged) / count)
    pooled = (jnp.stack(means, axis=2) - xf).astype(xp.dtype)
    y = jnp.einsum('bsgc,gcd->bsgd', pooled, w).reshape(b, s, POOL_CH) + bias
    return y * scale


def _spatial_gating(u, v, ln_g, ln_b, w_s, b_s):
    b, s, _ = v.shape
    vf = v.astype(jnp.float32)
    mu = jnp.mean(vf, axis=-1, keepdims=True)
    var = jnp.mean(jnp.square(vf - mu), axis=-1, keepdims=True)
    vn = ((vf - mu) * lax.rsqrt(var + EPS)).astype(v.dtype) * ln_g + ln_b
    nb = s // SGU_BLOCK
    vb = vn.reshape(b, nb, SGU_BLOCK, SGU_HEADS, SGU_HEAD_CH)
    chunk_id = jnp.arange(SGU_BLOCK) // CHUNK
    mask = chunk_id[:, None] >= chunk_id[None, :]
    w = jnp.where(mask[None], w_s, jnp.zeros_like(w_s))
    mixed = jnp.einsum('hij,bnjhc->bnihc', w, vb) + b_s.T[None, None, :, :, None]
    return u * mixed.reshape(b, s, SGU_CH)


def _short_conv_mixer(bg, cg, h, w):
    return bg * _causal_dwconv(cg * h, w)


def _ssd_mixer(z, xs, bm, cm, dt, conv_w, conv_b, dt_bias, a_log, d_skip, norm_g):
    dtype = z.dtype
    f32 = jnp.float32
    b, s, _ = xs.shape
    nc = s // SSD_CHUNK
    xbc = jax.nn.silu(_causal_dwconv(jnp.concatenate([xs, bm, cm], axis=-1), conv_w) + conv_b).astype(f32)
    xh = xbc[..., :SSD_INNER].reshape(b, nc, SSD_CHUNK, SSD_HEADS, SSD_HEADDIM)
    rep = SSD_HEADS // SSD_GROUPS
    bh = jnp.repeat(xbc[..., SSD_INNER:SSD_INNER + SSD_BC].reshape(b, nc, SSD_CHUNK, SSD_GROUPS, SSD_STATE), rep, axis=3)
    ch = jnp.repeat(xbc[..., SSD_INNER + SSD_BC:].reshape(b, nc, SSD_CHUNK, SSD_GROUPS, SSD_STATE), rep, axis=3)
    delta = jax.nn.softplus(dt.astype(f32) + dt_bias.astype(f32))
    a = -jnp.exp(a_log.astype(f32))
    delta_c = delta.reshape(b, nc, SSD_CHUNK, SSD_HEADS)
    a_cs = jnp.cumsum((delta_c * a).transpose(0, 3, 1, 2), axis=-1)
    x_dt = xh * delta_c[..., None]
    causal = jnp.tril(jnp.ones((SSD_CHUNK, SSD_CHUNK), dtype=bool))
    seg = a_cs[..., :, None] - a_cs[..., None, :]
    decay = jnp.exp(jnp.where(causal, seg, -jnp.inf))
    scores = jnp.einsum('bclhn,bcshn->bhcls', ch, bh) * decay
    y_diag = jnp.einsum('bhcls,bcshp->bclhp', scores, x_dt)
    to_end = jnp.exp(a_cs[..., -1:] - a_cs)
    states = jnp.einsum('bclhn,bhcl,bclhp->bchpn', bh, to_end, x_dt)
    chunk_decay = jnp.exp(a_cs[..., -1])

    def step(carry, inp):
        st, dec = inp
        return carry * dec[..., None, None] + st, carry

    init = jnp.zeros((b, SSD_HEADS, SSD_HEADDIM, SSD_STATE), f32)
    _, prev = lax.scan(step, init, (states.transpose(1, 0, 2, 3, 4), chunk_decay.transpose(2, 0, 1)))
    prev = prev.transpose(1, 0, 2, 3, 4)
    y_off = jnp.einsum('bclhn,bchpn,bhcl->bclhp', ch, prev, jnp.exp(a_cs))
    y = y_diag + y_off + xh * d_skip.astype(f32)[:, None]
    y = y.reshape(b, s, SSD_INNER) * jax.nn.silu(z.astype(f32))
    yg = y.reshape(b, s, SSD_GROUPS, SSD_INNER // SSD_GROUPS)
    yg = yg * lax.rsqrt(jnp.mean(jnp.square(yg), axis=-1, keepdims=True) + EPS)
    return (yg.reshape(b, s, SSD_INNER) * norm_g.astype(f32)).astype(dtype)


def setup_inputs(seed: int = 0) -> dict:
    key = jax.random.key(seed)
    ks = jax.random.split(key, 24)

    def nrm(k, shape, scale):
        return jax.random.normal(k, shape, jnp.float32) * scale

    L = DEPTH
    u = jax.random.uniform(ks[14], (L, SSD_HEADS), jnp.float32)
    dt0 = jnp.exp(u * (math.log(0.1) - math.log(1e-3)) + math.log(1e-3))
    dt_bias = dt0 + jnp.log(-jnp.expm1(-dt0))
    a_init = jax.random.uniform(ks[15], (L, SSD_HEADS), jnp.float32, 1.0, 16.0)
    return {
        'x': nrm(ks[0], (BATCH, SEQ, D_MODEL), 1.0),
        'norm_mix_g': 1.0 + nrm(ks[1], (L, D_MODEL), 0.02),
        'w_in': nrm(ks[2], (L, D_MODEL, IN_COLS), D_MODEL ** -0.5),
        'pool_w': nrm(ks[3], (L, POOL_GROUPS, POOL_GROUP_CH, POOL_GROUP_CH), POOL_GROUP_CH ** -0.5),
        'pool_b': nrm(ks[4], (L, POOL_CH), 0.02),
        'pool_scale': 1.0 + nrm(ks[5], (L, POOL_CH), 0.1),
        'sgu_ln_g': 1.0 + nrm(ks[6], (L, SGU_CH), 0.02),
        'sgu_ln_b': nrm(ks[7], (L, SGU_CH), 0.02),
        'sgu_w': nrm(ks[8], (L, SGU_HEADS, SGU_BLOCK, SGU_BLOCK), SGU_BLOCK ** -0.5),
        'sgu_b': 1.0 + nrm(ks[9], (L, SGU_HEADS, SGU_BLOCK), 0.1),
        'sconv_w': nrm(ks[10], (L, SCONV_WIDTH, SCONV_CH), SCONV_WIDTH ** -0.5),
        'ssd_conv_w': nrm(ks[11], (L, SSD_CONV, SSD_CONV_CH), SSD_CONV ** -0.5),
        'ssd_conv_b': nrm(ks[12], (L, SSD_CONV_CH), 0.02),
        'ssd_dt_bias': dt_bias,
        'ssd_a_log': jnp.log(a_init),
        'ssd_d': 1.0 + nrm(ks[13], (L, SSD_HEADS), 0.1),
        'ssd_norm_g': 1.0 + nrm(ks[16], (L, SSD_INNER), 0.02),
        'w_out': nrm(ks[17], (L, D_MIX, D_MODEL), D_MIX ** -0.5),
        'norm_ffn_g': 1.0 + nrm(ks[18], (L, D_MODEL), 0.02),
        'w_gate': nrm(ks[19], (L, D_MODEL, D_FF), D_MODEL ** -0.5),
        'w_up': nrm(ks[20], (L, D_MODEL, D_FF), D_MODEL ** -0.5),
        'w_down': nrm(ks[21], (L, D_FF, D_MODEL), D_FF ** -0.5),
        'final_norm_g': 1.0 + nrm(ks[22], (D_MODEL,), 0.02),
    }


def reference(x, norm_mix_g, w_in, pool_w, pool_b, pool_scale, sgu_ln_g, sgu_ln_b, sgu_w, sgu_b,
              sconv_w, ssd_conv_w, ssd_conv_b, ssd_dt_bias, ssd_a_log, ssd_d, ssd_norm_g,
              w_out, norm_ffn_g, w_gate, w_up, w_down, final_norm_g):
    h = x
    for l in range(DEPTH):
        hn = _rmsnorm(h, norm_mix_g[l])
        proj = jnp.einsum('bsd,de->bse', hn, w_in[l])
        (p_pool, p_u, p_v, p_cb, p_cc, p_ch,
         p_z, p_x, p_b, p_c, p_dt) = _split_columns(proj)
        y_a = _pool_mixer(p_pool, pool_w[l], pool_b[l], pool_scale[l])
        y_b = _spatial_gating(p_u, p_v, sgu_ln_g[l], sgu_ln_b[l], sgu_w[l], sgu_b[l])
        y_c = _short_conv_mixer(p_cb, p_cc, p_ch, sconv_w[l])
        y_d = _ssd_mixer(p_z, p_x, p_b, p_c, p_dt, ssd_conv_w[l], ssd_conv_b[l],
                         ssd_dt_bias[l], ssd_a_log[l], ssd_d[l], ssd_norm_g[l])
        mix = jnp.concatenate([y_a, y_b, y_c, y_d], axis=-1)
        h = h + jnp.einsum('bse,ed->bsd', mix, w_out[l])
        hn = _rmsnorm(h, norm_ffn_g[l])
        gate = jax.nn.silu(jnp.einsum('bsd,df->bsf', hn, w_gate[l]))
        up = jnp.einsum('bsd,df->bsf', hn, w_up[l])
        h = h + jnp.einsum('bsf,fd->bsd', gate * up, w_down[l])
    return _rmsnorm(h, final_norm_g)
```

```python
import types
import numpy as np
import concourse.bass as bass
import concourse.mybir as mybir
from concourse.bass_utils import run_bass_kernel_spmd

F32 = mybir.dt.float32
BF16 = mybir.dt.bfloat16
AF = mybir.ActivationFunctionType
ALU = mybir.AluOpType

D = 1024
KC = 8
SEQ = 2048
DEPTH = 2
INC = 2564
DFF = 2816
NFC = 22
EPS = 1e-6
T = 512
PASS_TOK = 1024
NT = PASS_TOK // T
NB = T // 128
FGROUPS = [(0, 3), (3, 3), (6, 3), (9, 3), (12, 3), (15, 3), (18, 2), (20, 2)]
SLOT_ELEMS = 9760
NSLOT = 4
ASPLIT = (0, 1184, 2368, 2564)
POOL_WINS = (2, 4, 8, 16)
POOL_CONV_Q = ()
PRIO_RANK = True
DMA_MARGIN_NS = 15000.0
XLAT_NS = 600.0

PARAM_NAMES = ['norm_mix_g', 'w_in', 'pool_w', 'pool_b', 'pool_scale', 'sgu_ln_g', 'sgu_ln_b', 'sgu_w', 'sgu_b',
               'sconv_w', 'ssd_conv_w', 'ssd_conv_b', 'ssd_dt_bias', 'ssd_a_log', 'ssd_d', 'ssd_norm_g',
               'w_out', 'norm_ffn_g', 'w_gate', 'w_up', 'w_down', 'final_norm_g']


class Buf:
    def __init__(self, t, name, dram=False):
        self.t = t
        self.name = name
        self.dram = dram
        self.w = None
        self.wg = []
        self.r = []

    def __getitem__(self, k):
        return self.t[k]


def _freeze(fn):
    if fn is None or fn.__closure__ is None:
        return fn
    cells = []
    for c in fn.__closure__:
        try:
            cells.append(types.CellType(c.cell_contents))
        except ValueError:
            cells.append(c)
    return types.FunctionType(fn.__code__, fn.__globals__, fn.__name__, fn.__defaults__, tuple(cells))


class Op:
    __slots__ = ('i', 'e', 'fn', 'kind', 'preds', 'succs', 'cost', 'tbl', 'lat', 'semkey', 'semval', 'start', 'fin', 'sbuf', 'nb')


class FW:
    def __init__(self, nc):
        self.nc = nc
        self.eng = {'pe': nc.tensor, 'act': nc.scalar, 'dve': nc.vector, 'pool': nc.gpsimd, 'sp': nc.sync}
        self.ops = []
        self.skip = False
        self.cnt = {k: 0 for k in self.eng}
        self.nwait = 0

    def sb(self, name, shape, dt=F32):
        return Buf(self.nc.alloc_sbuf_tensor(name, list(shape), dt), name)

    def ps(self, name, shape, dt=F32):
        return Buf(self.nc.alloc_psum_tensor(name, list(shape), dt), name)

    def _record(self, e, fn, kind, reads, writes, cost, tbl, lat=0.0, sbuf=None, nb=0):
        o = Op()
        o.i = len(self.ops); o.e = e; o.fn = _freeze(fn); o.kind = kind; o.cost = cost; o.tbl = tbl; o.lat = lat
        o.sbuf = sbuf; o.nb = nb; o.succs = []; o.semkey = None; o.semval = 0; o.start = 0.0; o.fin = 0.0
        preds = set()
        reads = [b for b in reads if not b.dram]
        writes = [b for b in writes if not b.dram]
        grouped = set()
        for b in reads:
            if b.w is not None:
                preds.add(b.w)
                preds.update(b.wg)
        for b in writes:
            if b.w is not None:
                if kind == 'dma' and self.ops[b.w].kind == 'dma' and self.ops[b.w].sbuf is sbuf and b is sbuf and not b.r:
                    grouped.add(id(b))
                    for w0 in b.wg[:1]:
                        preds.update(self.ops[w0].preds)
                else:
                    preds.add(b.w)
                    preds.update(b.wg)
            for r in b.r:
                preds.add(r)
        preds.discard(o.i)
        o.preds = sorted(preds)
        self.ops.append(o)
        for b in reads:
            b.r.append(o.i)
        for b in writes:
            if id(b) in grouped:
                b.wg.append(o.i)
            else:
                b.wg = [o.i] if kind == 'dma' else []
            b.w = o.i
            b.r = []
        return o

    def op(self, e, ins_fn, reads=(), writes=(), n=512, tbl=None):
        if self.skip:
            return None
        if e == 'act' and tbl is None:
            nm = ins_fn.__code__.co_names
            tbl = 'S' if 'Silu' in nm else ('E' if ('Exp' in nm or 'Ln' in nm) else None)
        if e == 'pe':
            cost = max(n, 64) / 2.4 + 12.0
        elif e == 'act':
            cost = n * 0.84 + 200.0
        elif e == 'dve':
            cost = n * 1.05 + 110.0
        else:
            cost = n * 2.0 + 200.0
        return self._record(e, ins_fn, 'c', reads, writes, cost, tbl)

    def dma(self, q, out_ap, in_ap, reads=(), writes=(), nbytes=262144, **kw):
        if self.skip:
            return None
        sb_side = None
        for b in list(writes) + list(reads):
            if not b.dram:
                sb_side = b
                break
        fn = (lambda: self.eng[q].dma_start(out=out_ap, in_=in_ap, **kw))
        issue = 1500.0 if q == 'pool' else 150.0
        is_store = any(b.dram for b in writes)
        return self._record(q, fn, 'dma', reads, writes, issue, None, lat=2500.0 + nbytes / 150.0, sbuf=sb_side, nb=(-1 if is_store else 0))

    def schedule(self):
        ops = self.ops
        N = len(ops)
        indeg = [len(o.preds) for o in ops]
        for o in ops:
            for p in o.preds:
                ops[p].succs.append(o.i)
        rank = [0.0] * N
        for o in reversed(ops):
            m = 0.0
            for s_ in o.succs:
                if rank[s_] > m:
                    m = rank[s_]
            rank[o.i] = o.cost + o.lat + m
        avail = {k: [] for k in self.eng}
        free = {k: 0.0 for k in self.eng}
        cur_tbl = [None]
        order = {k: [] for k in self.eng}
        for o in ops:
            if indeg[o.i] == 0:
                avail[o.e].append((0.0, o.i))
        done = 0
        LOOK = 400
        minidx = {k: 0 for k in self.eng}
        while done < N:
            best = None
            for k, lst in avail.items():
                if not lst:
                    continue
                fr = free[k]
                mr = min(lst)[0]
                t = fr if fr > mr else mr
                ci = None
                if PRIO_RANK:
                    br = -1.0
                    for (rt, idx) in lst:
                        if rt <= t and (rank[idx] > br):
                            br = rank[idx]; ci = idx
                else:
                    ck = None
                    for (rt, idx) in lst:
                        if rt <= t:
                            kk = (0 if ops[idx].kind == 'dma' else 1, idx)
                            if ck is None or kk < ck:
                                ck = kk; ci = idx
                if best is None or t < best[0] or (t == best[0] and ci < best[2]):
                    best = (t, k, ci)
            t, k, ci = best
            o = ops[ci]
            lst = avail[k]
            for j, (rt, idx) in enumerate(lst):
                if idx == ci:
                    lst.pop(j)
                    break
            c = o.cost
            if k == 'act' and o.tbl is not None and o.tbl != cur_tbl[0]:
                c += 2700.0
                cur_tbl[0] = o.tbl
            o.start = t
            free[k] = t + c
            o.fin = t + c + o.lat
            order[k].append(ci)
            done += 1
            for s_ in o.succs:
                indeg[s_] -= 1
                if indeg[s_] == 0:
                    so = ops[s_]
                    rt = max(ops[p].fin + (0.0 if ops[p].e == so.e else XLAT_NS) for p in so.preds)
                    if so.kind == 'dma' and so.e == 'pool' and rt > 0.0:
                        rt += DMA_MARGIN_NS
                    avail[so.e].append((rt, s_))
        self.order = order
        self.makespan = max(o.fin for o in ops)

    def emit(self):
        nc = self.nc
        ops = self.ops
        sem = {k: nc.alloc_semaphore('s_' + k) for k in self.eng}
        dsem = {}
        dcnt = {}
        for k, lst in self.order.items():
            c = 0
            for ci in lst:
                o = ops[ci]
                if o.kind == 'dma':
                    key = id(o.sbuf)
                    if key not in dsem:
                        dsem[key] = nc.alloc_semaphore('d%d_%s' % (len(dsem), o.sbuf.name))
                        dcnt[key] = 0
                    dcnt[key] += 16
                    o.semkey = ('d', key); o.semval = dcnt[key]
                else:
                    c += 1
                    o.semkey = ('e', k); o.semval = c
            self.cnt[k] = c
        for k, lst in self.order.items():
            eng = self.eng[k]
            seen = {}
            for ci in lst:
                o = ops[ci]
                need = {}
                for p in o.preds:
                    po = ops[p]
                    if po.semkey is None:
                        continue
                    if k == 'pe' and po.semkey == ('e', 'pe'):
                        continue
                    if need.get(po.semkey, 0) < po.semval:
                        need[po.semkey] = po.semval
                todo = [(key, v) for key, v in need.items() if seen.get(key, 0) < v]
                attach = None
                if todo and o.kind != 'wait':
                    attach = todo.pop()
                for key, v in todo:
                    sh = sem[key[1]] if key[0] == 'e' else dsem[key[1]]
                    eng.wait_ge(sh, v)
                    self.nwait += 1
                    seen[key] = v
                if o.kind == 'wait':
                    continue
                ins = o.fn()
                if attach is not None:
                    key, v = attach
                    ins._wait_ge(sem[key[1]] if key[0] == 'e' else dsem[key[1]], v)
                    seen[key] = v
                if o.kind == 'dma':
                    ins.then_inc(dsem[o.semkey[1]], 16)
                else:
                    ins.then_inc(sem[k], 1)

    def finish(self, e='sp'):
        o = Op()
        o.i = len(self.ops); o.e = e; o.fn = None; o.kind = 'wait'; o.cost = 10.0; o.tbl = None; o.lat = 0.0
        o.sbuf = None; o.nb = 0; o.succs = []; o.semkey = None; o.semval = 0; o.start = 0.0; o.fin = 0.0
        o.preds = [x.i for x in self.ops if x.kind == 'dma' and x.nb == -1]
        self.ops.append(o)
        return o


def build(npass=4, nlayers=DEPTH, final=True, dbg=None, stage=99):
    nc = bass.Bass("TRN2", target_bir_lowering=False)
    f = FW(nc)
    V, S_, P_ = nc.vector, nc.scalar, nc.tensor

    shapes = {
        'x': [2, SEQ, D], 'norm_mix_g': [DEPTH, D], 'w_in': [DEPTH, D, INC], 'pool_w': [DEPTH, 4, 64, 64],
        'pool_b': [DEPTH, 256], 'pool_scale': [DEPTH, 256], 'sgu_ln_g': [DEPTH, 256], 'sgu_ln_b': [DEPTH, 256],
        'sgu_w': [DEPTH, 4, 128, 128], 'sgu_b': [DEPTH, 4, 128], 'sconv_w': [DEPTH, 3, 256],
        'ssd_conv_w': [DEPTH, 4, 768], 'ssd_conv_b': [DEPTH, 768], 'ssd_dt_bias': [DEPTH, 4], 'ssd_a_log': [DEPTH, 4],
        'ssd_d': [DEPTH, 4], 'ssd_norm_g': [DEPTH, 256], 'w_out': [DEPTH, D, D], 'norm_ffn_g': [DEPTH, D],
        'w_gate': [DEPTH, D, DFF], 'w_up': [DEPTH, D, DFF], 'w_down': [DEPTH, DFF, D], 'final_norm_g': [D],
    }
    dr = {}
    for n, s in shapes.items():
        dr[n] = nc.dram_tensor(n, s, F32, kind="ExternalInput").ap()
    out_d = nc.dram_tensor("out", [2, SEQ, D], F32, kind="ExternalOutput").ap()
    wdram = Buf(None, 'wdram', dram=True)
    outb = Buf(None, 'outb', dram=True)
    dbg_aps = {}
    if dbg:
        for n, s in dbg.items():
            dbg_aps[n] = nc.dram_tensor('dbg_' + n, s, F32, kind="ExternalOutput").ap()
    dbgb = Buf(None, 'dbgb', dram=True)

    ones_f = f.sb('ones_f', [128, 128]); ones_b = f.sb('ones_b', [128, 128], BF16)
    ident_f = f.sb('ident_f', [128, 128]); ident_b = f.sb('ident_b', [128, 128], BF16)
    lmask = f.sb('lmask', [128, 128]); umask = f.sb('umask', [128, 128])
    negm = f.sb('negm', [128, 512], BF16)
    invwin = f.sb('invwin', [128, 2]); invcnt = f.sb('invcnt', [128, 2, 16])
    G = nc.gpsimd
    f.op('pool', lambda: G.memset(ones_f[:, :], 1.0), writes=[ones_f])
    f.op('pool', lambda: G.memset(ones_b[:, :], 1.0), writes=[ones_b])
    f.op('pool', lambda: G.memset(ident_f[:, :], 1.0), writes=[ident_f])
    f.op('pool', lambda: G.affine_select(out=ident_f[:, :], in_=ident_f[:, :], pattern=[[-1, 128]],
                                         compare_op=ALU.is_equal, fill=0.0, base=0, channel_multiplier=1),
         reads=[ident_f], writes=[ident_f])
    f.op('pool', lambda: G.tensor_copy(out=ident_b[:, :], in_=ident_f[:, :]), reads=[ident_f], writes=[ident_b])
    f.op('pool', lambda: G.memset(lmask[:, :], 1.0), writes=[lmask])
    f.op('pool', lambda: G.affine_select(out=lmask[:, :], in_=lmask[:, :], pattern=[[1, 128]],
                                         compare_op=ALU.is_ge, fill=0.0, base=0, channel_multiplier=-1),
         reads=[lmask], writes=[lmask])
    f.op('pool', lambda: G.memset(umask[:, :], 1.0), writes=[umask])
    f.op('pool', lambda: G.affine_select(out=umask[:, :], in_=umask[:, :], pattern=[[-1, 128]],
                                         compare_op=ALU.is_gt, fill=0.0, base=0, channel_multiplier=1),
         reads=[umask], writes=[umask])
    for hd_ in range(4):
        f.op('pool', lambda hd_=hd_: G.tensor_scalar(out=negm[:, hd_ * 128:(hd_ + 1) * 128], in0=umask[:, :], scalar1=-16384.0, scalar2=None, op0=ALU.mult),
             reads=[umask], writes=[negm])
    for j in range(2):
        for hf in range(2):
            win = POOL_WINS[2 * j + hf]
            sl = slice(hf * 64, hf * 64 + 64)
            f.op('pool', lambda j=j, sl=sl, win=win: G.memset(invwin[sl, j:j + 1], 1.0 / win), writes=[invwin])
            f.op('pool', lambda j=j, sl=sl, win=win: G.memset(invcnt[sl, j, :], 1.0 / win), writes=[invcnt])
            for t in range(win - 1):
                f.op('pool', lambda j=j, sl=sl, t=t: G.memset(invcnt[sl, j, t:t + 1], 1.0 / (t + 1)), writes=[invcnt])

    LP = []
    for l in range(DEPTH):
        p = {}
        p['g1'] = f.sb('g1_%d' % l, [128, KC]); p['g2'] = f.sb('g2_%d' % l, [128, KC])
        p['pwb'] = f.sb('pwb_%d' % l, [128, 2, 128], BF16)
        p['pb'] = f.sb('pb_%d' % l, [128, 2]); p['psc'] = f.sb('psc_%d' % l, [128, 2])
        p['lng'] = f.sb('lng_%d' % l, [128, 256]); p['lnb'] = f.sb('lnb_%d' % l, [128, 256])
        p['wst'] = f.sb('wst_%d' % l, [128, 4, 128], BF16)
        p['bsb'] = f.sb('bsb_%d' % l, [128, 2, 128])
        p['scw'] = f.sb('scw_%d' % l, [128, 2, 3]); p['cw'] = f.sb('cw_%d' % l, [128, 6, 4]); p['cb'] = f.sb('cb_%d' % l, [128, 6])
        p['dtb'] = f.sb('dtb_%d' % l, [128, 4]); p['ab'] = f.sb('ab_%d' % l, [128, 4])
        p['dsk'] = f.sb('dsk_%d' % l, [128, 2]); p['ng'] = f.sb('ng_%d' % l, [128, 2])
        p['pool_halo'] = f.sb('phalo_%d' % l, [128, 2, 16])
        p['sc_halo'] = f.sb('schalo_%d' % l, [128, 2, 2])
        p['ssd_halo'] = f.sb('sshalo_%d' % l, [128, 6, 3])
        p['prev_f'] = f.sb('prevf_%d' % l, [128, 256])
        p['prev_b'] = f.sb('prevb_%d' % l, [128, 256], BF16)
        LP.append(p)
    gfin = f.sb('gfin', [128, KC])

    hT = [f.sb('h_%d' % t, [128, KC, T]) for t in range(NT)]
    hn2 = [f.sb('hn2_%d' % t, [128, KC, T], BF16) for t in range(NT)]
    mxs = [f.sb('mix_%d' % i, [128, KC, T], BF16) for i in range(2)]
    mxks = [[Buf(None, 'mxk_%d_%d' % (i, k)) for k in range(KC)] for i in range(2)]
    hn2k = [[Buf(None, 'hn2k_%d_%d' % (t, k)) for k in range(KC)] for t in range(NT)]
    hk = [[Buf(None, 'hk_%d_%d' % (t, k)) for k in range(KC)] for t in range(NT)]
    xk = [Buf(None, 'xk_%d' % q) for q in range(6)]
    zk = [Buf(None, 'zk_%d' % j) for j in range(2)]
    uk = [Buf(None, 'uk_%d' % j) for j in range(2)]
    yk = [Buf(None, 'yk_%d' % j) for j in range(2)]
    fk = [[Buf(None, 'fk_%d_%d' % (i, j)) for j in range(3)] for i in range(2)]
    slots = [f.sb('slot_%d' % i, [128, SLOT_ELEMS], BF16) for i in range(NSLOT)]
    banks = [f.ps('bank_%d' % i, [128, 512]) for i in range(8)]
    u_sb = f.sb('u_sb', [128, 2, T], BF16)
    zs = f.sb('zs', [128, 2, T], BF16)
    ytile = f.sb('ytile', [128, 2, T], F32)
    xbc = f.sb('xbc', [128, 6, T], BF16)
    ffacts = [f.sb('ffact_%d' % i, [128, 3, T], BF16) for i in range(2)]
    fring = [f.sb('fr_%d' % i, [128, 528], F32) for i in range(9)]
    bring = [f.sb('br_%d' % i, [128, 512], BF16) for i in range(8)]
    st = {'bank': 0, 'f': 0, 'b': 0}

    BANK_POOLS = {'in': (0, 1, 2, 3, 4, 5, 6, 7), 'ssd': (0, 1, 2, 3, 4, 5, 6, 7), 'misc': (0, 1, 2, 3, 4, 5, 6, 7), 'gu': (0, 1, 2), 'dn': (3, 4, 5, 6, 7)}
    bank_ctr = {k: 0 for k in BANK_POOLS}

    def bank(kind='in'):
        pool_ = BANK_POOLS[kind]
        b = banks[pool_[bank_ctr[kind] % len(pool_)]]
        bank_ctr[kind] += 1
        return b

    def Fr():
        b = fring[st['f'] % len(fring)]
        st['f'] += 1
        return b

    def Br():
        b = bring[st['b'] % len(bring)]
        st['b'] += 1
        return b

    rings = {}

    def ring(name, shape, dt=F32, n=2):
        if name not in rings:
            rings[name] = [[f.sb('%s_%d' % (name, i), shape, dt) for i in range(n)], 0]
        r = rings[name]
        b = r[0][r[1] % n]
        r[1] += 1
        return b

    def v4(buf):
        return buf[:, 0:512].rearrange("p (h l) -> p h l", h=4)

    stg = Fr()
    f.op('dve', lambda: V.memset(stg[:, 0:128], 0.0), writes=[stg])

    def rows(dst_r0, src_ap_2d, n):
        f.dma('sp', stg[dst_r0:dst_r0 + n, 0:128], src_ap_2d, writes=[stg])
    for l in range(DEPTH):
        base = 58 * l
        rows(base + 0, dr['norm_mix_g'][l].rearrange("(k p) -> k p", p=128), 8)
        rows(base + 8, dr['norm_ffn_g'][l].rearrange("(k p) -> k p", p=128), 8)
        rows(base + 16, dr['pool_b'][l].rearrange("(k p) -> k p", p=128), 2)
        rows(base + 18, dr['pool_scale'][l].rearrange("(k p) -> k p", p=128), 2)
        rows(base + 20, dr['sconv_w'][l].rearrange("k (j p) -> (k j) p", p=128), 6)
        rows(base + 26, dr['ssd_conv_w'][l].rearrange("k (q p) -> (k q) p", p=128), 24)
        rows(base + 50, dr['ssd_conv_b'][l].rearrange("(q p) -> q p", p=128), 6)
        rows(base + 56, dr['ssd_norm_g'][l].rearrange("(j p) -> j p", p=128), 2)
    rows(116, dr['final_norm_g'].rearrange("(k p) -> k p", p=128), 8)
    bkp = bank()
    f.op('pe', lambda: P_.transpose(bkp[:, 0:128], stg[:, 0:128], ident_f[:, :]), reads=[stg, ident_f], writes=[bkp], n=512)
    for l in range(DEPTH):
        p = LP[l]
        base = 58 * l
        cps = [(p['g1'][:, :], bkp[:, base:base + 8], p['g1']), (p['g2'][:, :], bkp[:, base + 8:base + 16], p['g2']),
               (p['pb'][:, :], bkp[:, base + 16:base + 18], p['pb']), (p['psc'][:, :], bkp[:, base + 18:base + 20], p['psc']),
               (p['scw'][:, :, :], bkp[:, base + 20:base + 26].rearrange("p (k j) -> p j k", k=3), p['scw']),
               (p['cw'][:, :, :], bkp[:, base + 26:base + 50].rearrange("p (k q) -> p q k", k=4), p['cw']),
               (p['cb'][:, :], bkp[:, base + 50:base + 56], p['cb']), (p['ng'][:, :], bkp[:, base + 56:base + 58], p['ng'])]
        for (o, i, bb) in cps:
            f.op('dve', lambda o=o, i=i: V.tensor_copy(out=o, in_=i), reads=[bkp], writes=[bb])
    f.op('dve', lambda: V.tensor_copy(out=gfin[:, :], in_=bkp[:, 116:124]), reads=[bkp], writes=[gfin])

    for l in range(DEPTH):
        p = LP[l]
        rA = Fr(); rB = Fr(); rC = Fr()
        f.dma('sp', rA[0:1, 0:256], dr['sgu_ln_g'][l:l + 1, :], writes=[rA])
        f.dma('sp', rA[0:1, 256:512], dr['sgu_ln_b'][l:l + 1, :], writes=[rA])
        f.dma('sp', rB[0:1, 0:512], dr['sgu_b'][l:l + 1].rearrange("o h i -> o (h i)"), writes=[rB])
        f.dma('sp', rC[0:1, 0:4], dr['ssd_dt_bias'][l:l + 1, :], writes=[rC])
        f.dma('sp', rC[0:1, 4:8], dr['ssd_a_log'][l:l + 1, :], writes=[rC])
        f.dma('sp', rC[0:1, 8:12], dr['ssd_d'][l:l + 1, :], writes=[rC])
        bA = bank(); bB = bank(); bC = bank()
        f.op('pe', lambda bA=bA, rA=rA: P_.matmul(bA[:, :], lhsT=ones_f[0:1, :], rhs=rA[0:1, 0:512], start=True, stop=True), reads=[ones_f, rA], writes=[bA])
        f.op('pe', lambda bB=bB, rB=rB: P_.matmul(bB[:, :], lhsT=ones_f[0:1, :], rhs=rB[0:1, 0:512], start=True, stop=True), reads=[ones_f, rB], writes=[bB])
        f.op('pe', lambda bC=bC, rC=rC: P_.matmul(bC[:, 0:12], lhsT=ones_f[0:1, :], rhs=rC[0:1, 0:12], start=True, stop=True), reads=[ones_f, rC], writes=[bC])
        f.op('dve', lambda p=p, bA=bA: V.tensor_copy(out=p['lng'][:, :], in_=bA[:, 0:256]), reads=[bA], writes=[p['lng']], n=256)
        f.op('dve', lambda p=p, bA=bA: V.tensor_copy(out=p['lnb'][:, :], in_=bA[:, 256:512]), reads=[bA], writes=[p['lnb']])
        for jj in range(2):
            for hf in range(2):
                hh = 2 * jj + hf
                f.op('dve', lambda p=p, bB=bB, jj=jj, hf=hf, hh=hh: V.tensor_copy(out=p['bsb'][hf * 64:hf * 64 + 64, jj, :], in_=bB[hf * 64:hf * 64 + 64, hh * 128:(hh + 1) * 128]),
                     reads=[bB], writes=[p['bsb']], n=256)
                f.op('dve', lambda p=p, bC=bC, jj=jj, hf=hf, hh=hh: V.tensor_copy(out=p['dsk'][hf * 64:hf * 64 + 64, jj:jj + 1], in_=bC[hf * 64:hf * 64 + 64, 8 + hh:9 + hh]),
                     reads=[bC], writes=[p['dsk']], n=16)
        f.op('dve', lambda p=p, bC=bC: V.tensor_copy(out=p['dtb'][:, :], in_=bC[:, 0:4]), reads=[bC], writes=[p['dtb']], n=16)
        f.op('dve', lambda p=p, bC=bC: V.tensor_copy(out=p['ab'][:, :], in_=bC[:, 4:8]), reads=[bC], writes=[p['ab']], n=16)
        f.op('act', lambda p=p: S_.activation(out=p['ab'][:, :], in_=p['ab'][:, :], func=AF.Exp), reads=[p['ab']], writes=[p['ab']], n=16)
        f.op('act', lambda p=p: S_.mul(out=p['ab'][:, :], in_=p['ab'][:, :], mul=-1.0), reads=[p['ab']], writes=[p['ab']], n=16)
        pwf = Fr()
        f.op('dve', lambda pwf=pwf: V.memset(pwf[:, 0:256], 0.0), writes=[pwf], n=256)
        for j in range(2):
            for hf in range(2):
                f.dma('sp', pwf[hf * 64:hf * 64 + 64, j * 128 + hf * 64:j * 128 + hf * 64 + 64], dr['pool_w'][l, 2 * j + hf], writes=[pwf])
        f.op('dve', lambda p=p, pwf=pwf: V.tensor_copy(out=p['pwb'][:, :, :], in_=pwf[:, 0:256].rearrange("p (j d) -> p j d", j=2)), reads=[pwf], writes=[p['pwb']], n=256)
        wsf = Fr()
        for hd in range(4):
            f.dma('sp', wsf[:, hd * 128:(hd + 1) * 128], dr['sgu_w'][l, hd], writes=[wsf])
        bk = bank()
        for hd in range(4):
            f.op('pe', lambda hd=hd, bk=bk, wsf=wsf: P_.transpose(bk[:, hd * 128:(hd + 1) * 128], wsf[:, hd * 128:(hd + 1) * 128], ident_f[:, :]),
                 reads=[wsf, ident_f], writes=[bk], n=512)
        f.op('dve', lambda bk=bk, p=p: V.tensor_copy(out=p['wst'][:, :, :], in_=bk[:, :].rearrange("p (h i) -> p h i", h=4)),
             reads=[bk], writes=[p['wst']])
        f.op('dve', lambda p=p: V.memset(p['wst'][64:128, :, 0:64], 0.0), reads=[], writes=[p['wst']])

    win_v = [dr['w_in'][l].rearrange("(k p) c -> p k c", p=128) for l in range(DEPTH)]
    wout_v = [dr['w_out'][l].rearrange("(k p) c -> p k c", p=128) for l in range(DEPTH)]
    wg_v = [dr['w_gate'][l].rearrange("(k p) c -> p k c", p=128) for l in range(DEPTH)]
    wu_v = [dr['w_up'][l].rearrange("(k p) c -> p k c", p=128) for l in range(DEPTH)]
    wd_v = [dr['w_down'][l].rearrange("(j p) c -> p j c", p=128) for l in range(DEPTH)]

    def view(slot, off, k, c):
        return slot.t[:, off:off + k * c].rearrange("p (k c) -> p k c", k=k)

    item_state = {'n': 0}

    def load_item(l, kind):
        slot = slots[item_state['n'] % NSLOT]
        item_state['n'] += 1
        it = {'slot': slot}
        if kind in ('A1', 'A2', 'A3'):
            ai = int(kind[1]) - 1
            c0, c1 = ASPLIT[ai], ASPLIT[ai + 1]
            v = view(slot, 0, KC, c1 - c0)
            f.dma('pool', v[:, :, :], win_v[l][:, :, c0:c1], writes=[slot], nbytes=KC * (c1 - c0) * 4 * 128)
            it['w'] = v; it['c0'] = c0; it['c1'] = c1
            if kind == 'A3':
                vo = view(slot, KC * (c1 - c0), KC, D)
                f.dma('pool', vo[:, :, :], wout_v[l][:, :, :], writes=[slot], nbytes=KC * D * 4 * 128)
                it['wo'] = vo
        else:
            c0, gcount = FGROUPS[kind]
            gw = gcount * 128
            vg = view(slot, 0, KC, gw)
            vu = view(slot, KC * gw, KC, gw)
            vd = view(slot, 2 * KC * gw, gcount, D)
            f.dma('pool', vg[:, :, :], wg_v[l][:, :, c0 * 128:c0 * 128 + gw], writes=[slot], nbytes=KC * gw * 4 * 128)
            f.dma('pool', vu[:, :, :], wu_v[l][:, :, c0 * 128:c0 * 128 + gw], writes=[slot], nbytes=KC * gw * 4 * 128)
            f.dma('pool', vd[:, :, :], wd_v[l][:, c0:c0 + gcount, :], writes=[slot], nbytes=gcount * D * 4 * 128)
            it['wg'] = vg; it['wu'] = vu; it['wd'] = vd; it['G'] = gcount
        return it

    sched = []
    for ps_i in range(npass):
        for l in range(nlayers):
            sched += [(l, 'A1'), (l, 'A2'), (l, 'A3')]
            for gi in range(len(FGROUPS)):
                sched.append((l, gi))
    loaded = []

    def ensure_loaded(upto):
        while len(loaded) <= upto and len(loaded) < len(sched):
            loaded.append(load_item(*sched[len(loaded)]))

    def rmsnorm_tile(src, dst, gvec, dstk):
        srck = hk[hT.index(src)]
        acc = bank('misc')
        for k in range(KC):
            sq = Br()
            if k % 4 == 3:
                f.op('dve', lambda k=k, sq=sq: V.tensor_tensor(out=sq[:, :], in0=src[:, k, :], in1=src[:, k, :], op=ALU.mult), reads=[srck[k]], writes=[sq])
            else:
                f.op('act', lambda k=k, sq=sq: S_.activation(out=sq[:, :], in_=src[:, k, :], func=AF.Square), reads=[srck[k]], writes=[sq])
            f.op('pe', lambda k=k, sq=sq: P_.matmul(acc[:, :], lhsT=ones_b[:, :], rhs=sq[:, :], start=(k == 0), stop=(k == KC - 1)),
                 reads=[ones_b, sq], writes=[acc])
        rs = Fr()
        f.op('act', lambda: S_.activation(out=rs[:, 0:T], in_=acc[:, :], func=AF.Ln, scale=1.0 / D, bias=EPS), reads=[acc], writes=[rs])
        f.op('act', lambda: S_.activation(out=rs[:, 0:T], in_=rs[:, 0:T], func=AF.Exp, scale=-0.5), reads=[rs], writes=[rs])
        for k in range(KC):
            f.op('dve', lambda k=k: V.scalar_tensor_tensor(out=dst[:, k, :], in0=src[:, k, :], scalar=gvec[:, k:k + 1], in1=rs[:, 0:T],
                                                           op0=ALU.mult, op1=ALU.mult), reads=[srck[k], gvec, rs], writes=[dstk[k]])

    def dump(name, src_ap, srcbuf):
        if name in dbg_aps:
            f.dma('sp', dbg_aps[name], src_ap, reads=[srcbuf], writes=[dbgb])

    item_idx = 0
    ensure_loaded(NSLOT - 1)
    for ps_i in range(npass):
        s_loc = ps_i // 2
        hp = ps_i % 2
        tok0 = hp * PASS_TOK
        for t in range(NT):
            for b in range(NB):
                r0 = tok0 + t * T + b * 128
                for half in range(2):
                    xin = Fr()
                    f.dma('sp', xin[:, 0:512], dr['x'][s_loc, r0:r0 + 128, half * 512:(half + 1) * 512], writes=[xin])
                    bk = bank()
                    for kk in range(4):
                        f.op('pe', lambda kk=kk, bk=bk, xin=xin: P_.transpose(bk[:, kk * 128:(kk + 1) * 128], xin[:, kk * 128:(kk + 1) * 128], ident_f[:, :]),
                             reads=[xin, ident_f], writes=[bk], n=512)
                    f.op('act', lambda half=half, bk=bk, t=t, b=b: S_.copy(out=hT[t][:, half * 4:half * 4 + 4, b * 128:(b + 1) * 128],
                                                                         in_=bk[:, :].rearrange("p (k c) -> p k c", k=4)),
                         reads=[bk], writes=[hk[t][half * 4 + i_] for i_ in range(4)])
        if hp == 0:
            for l in range(nlayers):
                p = LP[l]
                for nm in ('pool_halo', 'sc_halo', 'ssd_halo', 'prev_f', 'prev_b'):
                    buf = p[nm]
                    if len(buf.t.shape) == 3:
                        f.op('dve', lambda buf=buf: V.memset(buf[:, :, :], 0.0), writes=[buf])
                    else:
                        f.op('dve', lambda buf=buf: V.memset(buf[:, :], 0.0), writes=[buf])

        for l in range(nlayers):
            p = LP[l]
            itA = [loaded[item_idx], loaded[item_idx + 1], loaded[item_idx + 2]]
            WO, sWO = itA[2]['wo'], itA[2]['slot']

            def proj_fm(dst_bank, c0, src, n=128):
                srck = hn2k[hn2.index(src)]
                a = c0
                while a < c0 + n:
                    it = [i for i in itA if i['c0'] <= a < i['c1']][0]
                    e = min(c0 + n, it['c1'])
                    if (a - c0) % 128 != 0:
                        lim = (a - c0) & -(a - c0)
                        e = min(e, a + lim)
                    o0, o1 = a - c0, e - c0
                    for k in range(KC):
                        f.op('pe', lambda k=k, it=it, a=a, e=e, o0=o0, o1=o1: P_.matmul(dst_bank[o0:o1, :], lhsT=it['w'][:, k, a - it['c0']:e - it['c0']], rhs=src[:, k, :],
                                                                                      start=(k == 0), stop=(k == KC - 1)), reads=[it['slot'], srck[k]], writes=[dst_bank])
                    a = e

            def proj_tm(dst_ap, dst_bank, c0, n, src, b):
                it = [i for i in itA if i['c0'] <= c0 and c0 + n <= i['c1']][0]
                for k in range(KC):
                    f.op('pe', lambda k=k, it=it: P_.matmul(dst_ap, lhsT=src[:, k, b * 128:(b + 1) * 128], rhs=it['w'][:, k, c0 - it['c0']:c0 + n - it['c0']],
                                                          start=(k == 0), stop=(k == KC - 1)), reads=[it['slot'], hn2k[hn2.index(src)][k]], writes=[dst_bank], n=n)

            for t in range(NT):
                h = hT[t]
                hnb = hn2[t]
                mx = mxs[t % 2]
                mxk = mxks[t % 2]
                first_tile = (hp == 0 and t == 0)
                f.skip = False
                if t == 0:
                    rmsnorm_tile(h, hnb, p['g1'], hn2k[t])
                    for t2 in range(1, NT):
                        rmsnorm_tile(hT[t2], hn2[t2], p['g1'], hn2k[t2])

                f.skip = stage < 2
                for j in range(2):
                    xe = Fr(); sa = Fr(); sbb = Fr()
                    f.op('act', lambda j=j, xe=xe: S_.copy(out=xe[:, 0:16], in_=p['pool_halo'][:, j, :]), reads=[p['pool_halo']], writes=[xe], n=16)
                    bk = bank()
                    proj_fm(bk, j * 128, hnb)
                    f.op('act', lambda bk=bk, xe=xe: S_.copy(out=xe[:, 16:528], in_=bk[:, :]), reads=[bk], writes=[xe])
                    f.op('act', lambda j=j, xe=xe: S_.copy(out=p['pool_halo'][:, j, :], in_=xe[:, 512:528]), reads=[xe], writes=[p['pool_halo']], n=16)
                    f.op('dve', lambda xe=xe, sa=sa: V.tensor_tensor(out=sa[:, 1:528], in0=xe[:, 1:528], in1=xe[:, 0:527], op=ALU.add), reads=[xe], writes=[sa])
                    f.op('dve', lambda sa=sa, sbb=sbb: V.tensor_tensor(out=sbb[:, 3:528], in0=sa[:, 3:528], in1=sa[:, 1:526], op=ALU.add), reads=[sa], writes=[sbb])
                    if j == 1:
                        f.op('dve', lambda sa=sa, sbb=sbb: V.tensor_tensor(out=sa[:, 7:528], in0=sbb[:, 7:528], in1=sbb[:, 3:524], op=ALU.add), reads=[sbb, sa], writes=[sa])
                        f.op('dve', lambda sa=sa, sbb=sbb: V.tensor_tensor(out=sbb[64:128, 15:528], in0=sa[64:128, 15:528], in1=sa[64:128, 7:520], op=ALU.add),
                             reads=[sa, sbb], writes=[sbb])
                    pooled = Br()
                    for hf in range(2):
                        src = sa if hf == 0 else sbb
                        sl = slice(hf * 64, hf * 64 + 64)
                        f.op('dve', lambda j=j, sl=sl, src=src, xe=xe, pooled=pooled: V.scalar_tensor_tensor(out=pooled[sl, :], in0=src[sl, 16:528], scalar=invwin[sl, j:j + 1],
                                                                                                            in1=xe[sl, 16:528], op0=ALU.mult, op1=ALU.subtract),
                             reads=[src, xe, invwin], writes=[pooled])
                        if first_tile:
                            tmpc = ring('tmpc', [128, 16], F32, 2)
                            f.op('dve', lambda j=j, sl=sl, src=src, tmpc=tmpc: V.tensor_tensor(out=tmpc[sl, :], in0=src[sl, 16:32], in1=invcnt[sl, j, :], op=ALU.mult),
                                 reads=[src, invcnt], writes=[tmpc])
                            f.op('dve', lambda sl=sl, tmpc=tmpc, xe=xe, pooled=pooled: V.tensor_tensor(out=pooled[sl, 0:16], in0=tmpc[sl, :], in1=xe[sl, 16:32], op=ALU.subtract),
                                 reads=[tmpc, xe, pooled], writes=[pooled])
                    bk = bank()
                    f.op('pe', lambda j=j, bk=bk, pooled=pooled: P_.matmul(bk[:, :], lhsT=p['pwb'][:, j, :], rhs=pooled[:, :], start=True, stop=True),
                         reads=[p['pwb'], pooled], writes=[bk])
                    f.op('dve', lambda j=j, bk=bk: V.tensor_scalar(out=mx[:, j, :], in0=bk[:, :], scalar1=p['pb'][:, j:j + 1], scalar2=p['psc'][:, j:j + 1],
                                                                  op0=ALU.add, op1=ALU.mult), reads=[bk, p['pb'], p['psc']], writes=[mxk[j]])

                f.skip = stage < 3
                for j in range(2):
                    bk = bank()
                    proj_fm(bk, 256 + j * 128, hnb)
                    f.op('act', lambda j=j, bk=bk: S_.copy(out=u_sb[:, j, :], in_=bk[:, :]), reads=[bk], writes=[uk[j]])
                for b in range(NB):
                    bk = bank()
                    proj_tm(bk[:, 0:256], bk, 512, 256, hnb, b)
                    v_sb = Fr()
                    f.op('act', lambda bk=bk, v_sb=v_sb: S_.copy(out=v_sb[:, 0:256], in_=bk[:, 0:256]), reads=[bk], writes=[v_sb], n=256)
                    stt = ring('bnst', [128, 6], F32, 2)
                    mv = ring('bnmv', [128, 2], F32, 2)
                    f.op('dve', lambda stt=stt, v_sb=v_sb: V.bn_stats(out=stt[:, :], in_=v_sb[:, 0:256]), reads=[v_sb], writes=[stt], n=16)
                    f.op('dve', lambda stt=stt, mv=mv: V.bn_aggr(out=mv[:, :], in_=stt[:, :]), reads=[stt], writes=[mv], n=16)
                    rstd = ring('lnrstd', [128, 1], F32, 2)
                    f.op('act', lambda mv=mv, rstd=rstd: S_.activation(out=rstd[:, :], in_=mv[:, 1:2], func=AF.Ln, bias=EPS), reads=[mv], writes=[rstd], n=16)
                    f.op('act', lambda rstd=rstd: S_.activation(out=rstd[:, :], in_=rstd[:, :], func=AF.Exp, scale=-0.5), reads=[rstd], writes=[rstd], n=16)
                    vn = Fr()
                    f.op('dve', lambda v_sb=v_sb, mv=mv, rstd=rstd, vn=vn: V.tensor_scalar(out=vn[:, 0:256], in0=v_sb[:, 0:256], scalar1=mv[:, 0:1], scalar2=rstd[:, 0:1],
                                                                                        op0=ALU.subtract, op1=ALU.mult), reads=[v_sb, mv, rstd], writes=[vn], n=16)
                    f.op('dve', lambda vn=vn: V.tensor_tensor(out=vn[:, 0:256], in0=vn[:, 0:256], in1=p['lng'][:, :], op=ALU.mult), reads=[vn, p['lng']], writes=[vn], n=256)
                    vnb = Br()
                    f.op('dve', lambda vn=vn, vnb=vnb: V.tensor_tensor(out=vnb[:, 0:256], in0=vn[:, 0:256], in1=p['lnb'][:, :], op=ALU.add), reads=[vn, p['lnb']], writes=[vnb], n=256)
                    bk2 = bank()
                    for hd in range(4):
                        jj, hf = hd // 2, hd % 2
                        f.op('pe', lambda hd=hd, jj=jj, hf=hf, bk2=bk2, vnb=vnb: P_.matmul(bk2[hf * 64:hf * 64 + 64, jj * 128:(jj + 1) * 128],
                                                                                         lhsT=vnb[:, hd * 64:(hd + 1) * 64], rhs=p['wst'][:, hd, :], start=True, stop=True),
                             reads=[vnb, p['wst']], writes=[bk2], n=128)
                    sg = Fr()
                    f.op('dve', lambda bk2=bk2, sg=sg: V.tensor_tensor(out=sg[:, 0:256].rearrange("p (j i) -> p j i", j=2), in0=bk2[:, 0:256].rearrange("p (j i) -> p j i", j=2),
                                                                      in1=p['bsb'][:, :, :], op=ALU.add), reads=[bk2, p['bsb']], writes=[sg], n=256)
                    f.op('dve', lambda sg=sg, b=b: V.tensor_tensor(out=mx[:, 2:4, b * 128:(b + 1) * 128], in0=sg[:, 0:256].rearrange("p (j i) -> p j i", j=2),
                                                                  in1=u_sb[:, :, b * 128:(b + 1) * 128], op=ALU.mult), reads=[sg, uk[0], uk[1]], writes=[mxk[2], mxk[3]], n=256)

                f.skip = stage < 4
                for j in range(2):
                    pext = Fr()
                    f.op('act', lambda j=j, pext=pext: S_.copy(out=pext[:, 0:2], in_=p['sc_halo'][:, j, :]), reads=[p['sc_halo']], writes=[pext], n=16)
                    bk = bank()
                    proj_fm(bk, 1024 + j * 128, hnb)
                    cg = Fr()
                    f.op('act', lambda bk=bk, cg=cg: S_.copy(out=cg[:, 0:T], in_=bk[:, :]), reads=[bk], writes=[cg])
                    bk = bank()
                    proj_fm(bk, 1280 + j * 128, hnb)
                    f.op('dve', lambda bk=bk, cg=cg, pext=pext: V.tensor_tensor(out=pext[:, 2:514], in0=bk[:, :], in1=cg[:, 0:T], op=ALU.mult), reads=[bk, cg], writes=[pext])
                    f.op('act', lambda j=j, pext=pext: S_.copy(out=p['sc_halo'][:, j, :], in_=pext[:, 512:514]), reads=[pext], writes=[p['sc_halo']], n=16)
                    acc = Fr()
                    f.op('dve', lambda j=j, acc=acc, pext=pext: V.tensor_scalar(out=acc[:, 0:T], in0=pext[:, 2:514], scalar1=p['scw'][:, j, 2:3], scalar2=None, op0=ALU.mult),
                         reads=[pext, p['scw']], writes=[acc])
                    f.op('dve', lambda j=j, acc=acc, pext=pext: V.scalar_tensor_tensor(out=acc[:, 0:T], in0=pext[:, 1:513], scalar=p['scw'][:, j, 1:2], in1=acc[:, 0:T],
                                                                                      op0=ALU.mult, op1=ALU.add), reads=[pext, p['scw'], acc], writes=[acc])
                    f.op('dve', lambda j=j, acc=acc, pext=pext: V.scalar_tensor_tensor(out=acc[:, 0:T], in0=pext[:, 0:512], scalar=p['scw'][:, j, 0:1], in1=acc[:, 0:T],
                                                                                      op0=ALU.mult, op1=ALU.add), reads=[pext, p['scw'], acc], writes=[acc])
                    bk = bank()
                    proj_fm(bk, 768 + j * 128, hnb)
                    f.op('dve', lambda j=j, bk=bk, acc=acc: V.tensor_tensor(out=mx[:, 4 + j, :], in0=bk[:, :], in1=acc[:, 0:T], op=ALU.mult), reads=[bk, acc], writes=[mxk[4 + j]])

                f.skip = stage < 5.1
                for q in range(6):
                    bk = bank()
                    proj_fm(bk, 1792 + q * 128, hnb)
                    cin = Fr()
                    f.op('act', lambda q=q, cin=cin: S_.copy(out=cin[:, 0:3], in_=p['ssd_halo'][:, q, :]), reads=[p['ssd_halo']], writes=[cin], n=16)
                    f.op('act', lambda bk=bk, cin=cin: S_.copy(out=cin[:, 3:515], in_=bk[:, :]), reads=[bk], writes=[cin])
                    f.op('act', lambda q=q, cin=cin: S_.copy(out=p['ssd_halo'][:, q, :], in_=cin[:, 512:515]), reads=[cin], writes=[p['ssd_halo']], n=16)
                    acc = Fr()
                    ce, CE = ('pool', G) if q in POOL_CONV_Q else ('dve', V)
                    f.op(ce, lambda q=q, acc=acc, cin=cin, CE=CE: CE.tensor_scalar(out=acc[:, 0:T], in0=cin[:, 3:515], scalar1=p['cw'][:, q, 3:4], scalar2=None, op0=ALU.mult),
                         reads=[cin, p['cw']], writes=[acc])
                    for kk in (2, 1, 0):
                        f.op(ce, lambda q=q, kk=kk, acc=acc, cin=cin, CE=CE: CE.scalar_tensor_tensor(out=acc[:, 0:T], in0=cin[:, kk:kk + 512], scalar=p['cw'][:, q, kk:kk + 1], in1=acc[:, 0:T],
                                                                                                   op0=ALU.mult, op1=ALU.add), reads=[cin, p['cw'], acc], writes=[acc])
                    f.op('act', lambda q=q, acc=acc: S_.activation(out=xbc[:, q, :], in_=acc[:, 0:T], func=AF.Silu, bias=p['cb'][:, q:q + 1]),
                         reads=[acc, p['cb']], writes=[xk[q]])
                for j in range(2):
                    bk = bank()
                    proj_fm(bk, 1536 + j * 128, hnb)
                    f.op('act', lambda j=j, bk=bk: S_.activation(out=zs[:, j, :], in_=bk[:, :], func=AF.Silu), reads=[bk], writes=[zk[j]])
                f.skip = stage < 5.2
                bkd = bank()
                for b in range(NB):
                    proj_tm(bkd[:, b * 4:(b + 1) * 4], bkd, 2560, 4, hnb, b)
                delta = ring('delta', [128, NB, 4], F32, 1)
                da = ring('da', [128, NB, 4], F32, 1)
                for b in range(NB):
                    f.op('dve', lambda b=b: V.tensor_tensor(out=delta[:, b, :], in0=bkd[:, b * 4:(b + 1) * 4], in1=p['dtb'][:, :], op=ALU.add), reads=[bkd, p['dtb']], writes=[delta], n=16)
                f.op('act', lambda: S_.activation(out=delta[:, :, :], in_=delta[:, :, :], func=AF.Exp), reads=[delta], writes=[delta], n=16)
                f.op('act', lambda: S_.activation(out=delta[:, :, :], in_=delta[:, :, :], func=AF.Ln, bias=1.0), reads=[delta], writes=[delta], n=16)
                for b in range(NB):
                    f.op('dve', lambda b=b: V.tensor_tensor(out=da[:, b, :], in0=delta[:, b, :], in1=p['ab'][:, :], op=ALU.mult), reads=[delta, p['ab']], writes=[da], n=16)
                for b in range(NB):
                    bs = slice(b * 128, (b + 1) * 128)
                    f.skip = stage < 5.3
                    R = Fr()
                    f.op('dve', lambda R=R, b=b: V.tensor_tensor(out=v4(R), in0=lmask[:, :].unsqueeze(1).to_broadcast([128, 4, 128]),
                                                                 in1=da[:, b, :].unsqueeze(2).to_broadcast([128, 4, 128]), op=ALU.mult),
                         reads=[lmask, da], writes=[R], n=512)
                    bseg = bank('ssd')
                    f.op('pe', lambda bseg=bseg, R=R: P_.matmul(bseg[:, :], lhsT=umask[:, :], rhs=R[:, 0:512], start=True, stop=False), reads=[umask, R], writes=[bseg], n=2048)
                    f.op('pe', lambda bseg=bseg: P_.matmul(bseg[:, :], lhsT=ident_b[:, :], rhs=negm[:, :], start=False, stop=True),
                         reads=[ident_b, negm], writes=[bseg], n=512)
                    decay = Fr()
                    f.op('act', lambda bseg=bseg, decay=decay: S_.activation(out=decay[:, 0:512], in_=bseg[:, :], func=AF.Exp), reads=[bseg], writes=[decay])
                    bacs = bank('ssd')
                    f.op('pe', lambda bacs=bacs, R=R: P_.matmul(bacs[:, :], lhsT=ones_f[:, :], rhs=R[:, 0:512], start=True, stop=True), reads=[ones_f, R], writes=[bacs], n=2048)
                    eacs = Fr()
                    f.op('act', lambda bacs=bacs, eacs=eacs: S_.activation(out=eacs[:, 0:512], in_=bacs[:, :], func=AF.Exp), reads=[bacs], writes=[eacs])
                    bsm = bank('ssd')
                    f.op('pe', lambda bsm=bsm, b=b: P_.matmul(bsm[:, 0:4], lhsT=umask[:, :], rhs=da[:, b, :], start=True, stop=True), reads=[umask, da], writes=[bsm], n=64)
                    f.op('pe', lambda bsm=bsm, b=b: P_.matmul(bsm[:, 4:8], lhsT=ones_f[:, :], rhs=da[:, b, :], start=True, stop=True), reads=[ones_f, da], writes=[bsm], n=64)
                    tecd = ring('tecd', [128, 8], F32, 2)
                    f.op('act', lambda bsm=bsm, tecd=tecd: S_.activation(out=tecd[:, :], in_=bsm[:, 0:8], func=AF.Exp), reads=[bsm], writes=[tecd], n=16)
                    dte = ring('dte', [128, 4], F32, 2)
                    f.op('dve', lambda tecd=tecd, dte=dte, b=b: V.tensor_tensor(out=dte[:, :], in0=tecd[:, 0:4], in1=delta[:, b, :], op=ALU.mult), reads=[tecd, delta], writes=[dte], n=16)
                    f.skip = stage < 5.4
                    bankb = bank('ssd')
                    for q in range(4):
                        f.op('pe', lambda q=q, bs=bs, bankb=bankb: P_.matmul(bankb[:, q * 128:(q + 1) * 128], lhsT=xbc[:, q, bs], rhs=ident_b[:, :], start=True, stop=True), reads=[xk[q], ident_b], writes=[bankb], n=128)
                    xdt = Br(); xdte = Br(); btok = Br()
                    f.op('dve', lambda xdt=xdt, b=b, bankb=bankb: V.tensor_tensor(out=xdt[:, 0:256].rearrange("p (h d) -> p h d", h=4), in0=bankb[:, 0:256].rearrange("p (h d) -> p h d", h=4),
                                                                                in1=delta[:, b, :].unsqueeze(2).to_broadcast([128, 4, 64]), op=ALU.mult), reads=[bankb, delta], writes=[xdt], n=256)
                    f.op('dve', lambda xdte=xdte, dte=dte, bankb=bankb: V.tensor_tensor(out=xdte[:, 0:256].rearrange("p (h d) -> p h d", h=4), in0=bankb[:, 0:256].rearrange("p (h d) -> p h d", h=4),
                                                                                      in1=dte[:, :].unsqueeze(2).to_broadcast([128, 4, 64]), op=ALU.mult), reads=[bankb, dte], writes=[xdte], n=256)
                    f.op('dve', lambda btok=btok, bankb=bankb: V.tensor_copy(out=btok[:, 0:256], in_=bankb[:, 256:512]), reads=[bankb], writes=[btok], n=256)
                    f.skip = stage < 5.5
                    bsc = bank('ssd')
                    for g in range(2):
                        f.op('pe', lambda g=g, bsc=bsc, bs=bs: P_.matmul(bsc[:, g * 128:(g + 1) * 128], lhsT=xbc[:, 2 + g, bs], rhs=xbc[:, 4 + g, bs], start=True, stop=True),
                             reads=[xk[2 + g], xk[4 + g]], writes=[bsc], n=128)
                    scT = Br(); cp = Br()
                    for g in range(2):
                        f.op('dve', lambda g=g, bsc=bsc, scT=scT, decay=decay: V.tensor_tensor(out=scT[:, g * 256:(g + 1) * 256].rearrange("p (h l) -> p h l", h=2),
                                                                                             in0=bsc[:, g * 128:(g + 1) * 128].unsqueeze(1).to_broadcast([128, 2, 128]),
                                                                                             in1=decay[:, g * 256:(g + 1) * 256].rearrange("p (h l) -> p h l", h=2), op=ALU.mult),
                             reads=[bsc, decay], writes=[scT], n=256)
                        f.op('pool', lambda g=g, cp=cp, eacs=eacs, bs=bs: G.tensor_tensor(out=cp[:, g * 256:(g + 1) * 256].rearrange("p (h l) -> p h l", h=2),
                                                                                        in0=xbc[:, 4 + g, bs].unsqueeze(1).to_broadcast([128, 2, 128]),
                                                                                        in1=eacs[:, g * 256:(g + 1) * 256].rearrange("p (h l) -> p h l", h=2), op=ALU.mult),
                             reads=[xk[4 + g], eacs], writes=[cp], n=256)
                    f.skip = stage < 5.6
                    by = bank('ssd')
                    for hd in range(4):
                        jj, hf = hd // 2, hd % 2
                        osl = by[hf * 64:hf * 64 + 64, jj * 128:(jj + 1) * 128]
                        f.op('pe', lambda hd=hd, osl=osl, xdt=xdt, scT=scT: P_.matmul(osl, lhsT=xdt[:, hd * 64:(hd + 1) * 64], rhs=scT[:, hd * 128:(hd + 1) * 128], start=True, stop=False),
                             reads=[xdt, scT], writes=[by], n=128)
                        f.op('pe', lambda hd=hd, osl=osl, cp=cp: P_.matmul(osl, lhsT=p['prev_b'][:, hd * 64:(hd + 1) * 64], rhs=cp[:, hd * 128:(hd + 1) * 128], start=False, stop=True),
                             reads=[p['prev_b'], cp], writes=[by], n=128)
                    f.skip = stage < 5.7
                    bst = bank('ssd')
                    for g in range(2):
                        f.op('pe', lambda g=g, bst=bst, btok=btok, xdte=xdte: P_.matmul(bst[:, g * 128:(g + 1) * 128], lhsT=btok[:, g * 128:(g + 1) * 128], rhs=xdte[:, g * 128:(g + 1) * 128],
                                                                                      start=True, stop=True), reads=[btok, xdte], writes=[bst], n=128)
                    f.op('dve', lambda tecd=tecd: V.tensor_tensor(out=p['prev_f'][:, :].rearrange("p (h d) -> p h d", h=4), in0=p['prev_f'][:, :].rearrange("p (h d) -> p h d", h=4),
                                                                 in1=tecd[:, 4:8].unsqueeze(2).to_broadcast([128, 4, 64]), op=ALU.mult), reads=[p['prev_f'], tecd], writes=[p['prev_f']], n=256)
                    f.op('dve', lambda bst=bst: V.tensor_tensor(out=p['prev_f'][:, :], in0=p['prev_f'][:, :], in1=bst[:, 0:256], op=ALU.add), reads=[p['prev_f'], bst], writes=[p['prev_f']], n=256)
                    f.op('act', lambda: S_.copy(out=p['prev_b'][:, :], in_=p['prev_f'][:, :]), reads=[p['prev_f']], writes=[p['prev_b']])
                    for j in range(2):
                        f.op('dve', lambda j=j, by=by, bs=bs: V.scalar_tensor_tensor(out=ytile[:, j, bs], in0=xbc[:, j, bs], scalar=p['dsk'][:, j:j + 1], in1=by[:, j * 128:(j + 1) * 128],
                                                                                   op0=ALU.mult, op1=ALU.add), reads=[xk[j], p['dsk'], by], writes=[yk[j]], n=16)
                for j in range(2):
                    f.op('dve', lambda j=j: V.tensor_tensor(out=ytile[:, j, :], in0=ytile[:, j, :], in1=zs[:, j, :], op=ALU.mult), reads=[yk[j], zk[j]], writes=[yk[j]])
                    sq = Br()
                    f.op('act', lambda j=j, sq=sq: S_.activation(out=sq[:, :], in_=ytile[:, j, :], func=AF.Square), reads=[yk[j]], writes=[sq])
                    bk = bank('misc')
                    f.op('pe', lambda bk=bk, sq=sq: P_.matmul(bk[:, :], lhsT=ones_b[:, :], rhs=sq[:, :], start=True, stop=True), reads=[ones_b, sq], writes=[bk])
                    rs = Fr()
                    f.op('act', lambda bk=bk, rs=rs: S_.activation(out=rs[:, 0:T], in_=bk[:, :], func=AF.Ln, scale=1.0 / 128, bias=EPS), reads=[bk], writes=[rs])
                    f.op('act', lambda rs=rs: S_.activation(out=rs[:, 0:T], in_=rs[:, 0:T], func=AF.Exp, scale=-0.5), reads=[rs], writes=[rs])
                    f.op('dve', lambda j=j, rs=rs: V.scalar_tensor_tensor(out=mx[:, 6 + j, :], in0=ytile[:, j, :], scalar=p['ng'][:, j:j + 1], in1=rs[:, 0:T], op0=ALU.mult, op1=ALU.mult),
                         reads=[yk[j], p['ng'], rs], writes=[mxk[6 + j]])
                f.skip = False
                if dbg and ps_i == 0 and l == 0 and t == 0:
                    for k in range(KC):
                        mxf = Fr()
                        f.op('dve', lambda k=k, mxf=mxf: V.tensor_copy(out=mxf[:, 0:T], in_=mx[:, k, :]), reads=[mxk[k]], writes=[mxf])
                        dump('mix', mxf[:, 0:T], mxf) if False else None
                        if 'mix' in dbg_aps:
                            f.dma('sp', dbg_aps['mix'][:, k, :], mxf[:, 0:T], reads=[mxf], writes=[dbgb])

                f.skip = stage < 6
                for m in range(KC):
                    bk = bank('misc')
                    for k in range(KC):
                        f.op('pe', lambda k=k, m=m, bk=bk: P_.matmul(bk[:, :], lhsT=WO[:, k, m * 128:(m + 1) * 128], rhs=mx[:, k, :], start=(k == 0), stop=(k == KC - 1)),
                             reads=[sWO, mxk[k]], writes=[bk])
                    f.op('dve', lambda m=m, bk=bk: V.tensor_tensor(out=h[:, m, :], in0=h[:, m, :], in1=bk[:, :], op=ALU.add), reads=[hk[t][m], bk], writes=[hk[t][m]])
                rmsnorm_tile(h, hn2[t], p['g2'], hn2k[t])
            item_idx += 3
            ensure_loaded(item_idx + NSLOT - 1)

            f.skip = stage < 7
            for gi in range(len(FGROUPS)):
                it = loaded[item_idx]
                slot = it['slot']
                Gc = it['G']
                for t in range(NT):
                    h = hT[t]
                    ffact = ffacts[t % 2]
                    for j in range(Gc):
                        bg = bank('gu'); bu = bank('gu')
                        for k in range(KC):
                            f.op('pe', lambda k=k, j=j, bg=bg: P_.matmul(bg[:, :], lhsT=it['wg'][:, k, j * 128:(j + 1) * 128], rhs=hn2[t][:, k, :], start=(k == 0), stop=(k == KC - 1)),
                                 reads=[slot, hn2k[t][k]], writes=[bg])
                        for k in range(KC):
                            f.op('pe', lambda k=k, j=j, bu=bu: P_.matmul(bu[:, :], lhsT=it['wu'][:, k, j * 128:(j + 1) * 128], rhs=hn2[t][:, k, :], start=(k == 0), stop=(k == KC - 1)),
                                 reads=[slot, hn2k[t][k]], writes=[bu])
                        sg = Fr()
                        f.op('act', lambda bg=bg, sg=sg: S_.activation(out=sg[:, 0:T], in_=bg[:, :], func=AF.Silu), reads=[bg], writes=[sg])
                        f.op('dve', lambda j=j, bu=bu, sg=sg: V.tensor_tensor(out=ffact[:, j, :], in0=bu[:, :], in1=sg[:, 0:T], op=ALU.mult), reads=[bu, sg], writes=[fk[t % 2][j]])
                    for m in range(KC):
                        bk = bank('dn')
                        for j in range(Gc):
                            f.op('pe', lambda j=j, m=m, bk=bk: P_.matmul(bk[:, :], lhsT=it['wd'][:, j, m * 128:(m + 1) * 128], rhs=ffact[:, j, :], start=(j == 0), stop=(j == Gc - 1)),
                                 reads=[slot, fk[t % 2][j]], writes=[bk])
                        f.op('dve', lambda m=m, bk=bk, h=h: V.tensor_tensor(out=h[:, m, :], in0=h[:, m, :], in1=bk[:, :], op=ALU.add), reads=[hk[t][m], bk], writes=[hk[t][m]])
                item_idx += 1
                ensure_loaded(item_idx + NSLOT - 1)
            if dbg and ps_i == 0 and l == 0 and 'h0' in dbg_aps:
                f.dma('sp', dbg_aps['h0'], hT[0][:, :, :], reads=hk[0], writes=[dbgb])

        f.skip = False
        for t in range(NT):
            h = hT[t]
            if final:
                acc = bank('misc')
                for k in range(KC):
                    sq = Br()
                    f.op('act', lambda k=k, sq=sq: S_.activation(out=sq[:, :], in_=h[:, k, :], func=AF.Square), reads=[hk[t][k]], writes=[sq])
                    f.op('pe', lambda k=k, sq=sq, acc=acc: P_.matmul(acc[:, :], lhsT=ones_b[:, :], rhs=sq[:, :], start=(k == 0), stop=(k == KC - 1)), reads=[ones_b, sq], writes=[acc])
                rs = Fr()
                f.op('act', lambda acc=acc, rs=rs: S_.activation(out=rs[:, 0:T], in_=acc[:, :], func=AF.Ln, scale=1.0 / D, bias=EPS), reads=[acc], writes=[rs])
                f.op('act', lambda rs=rs: S_.activation(out=rs[:, 0:T], in_=rs[:, 0:T], func=AF.Exp, scale=-0.5), reads=[rs], writes=[rs])
                for k in range(KC):
                    f.op('dve', lambda k=k, rs=rs: V.scalar_tensor_tensor(out=h[:, k, :], in0=h[:, k, :], scalar=gfin[:, k:k + 1], in1=rs[:, 0:T], op0=ALU.mult, op1=ALU.mult),
                         reads=[hk[t][k], gfin, rs], writes=[hk[t][k]])
            for b in range(NB):
                r0 = tok0 + t * T + b * 128
                for half in range(2):
                    bk = bank('misc')
                    for kk in range(4):
                        k = half * 4 + kk
                        f.op('pe', lambda k=k, kk=kk, bk=bk, b=b: P_.transpose(bk[:, kk * 128:(kk + 1) * 128], h[:, k, b * 128:(b + 1) * 128], ident_f[:, :]),
                             reads=[hk[t][k], ident_f], writes=[bk], n=512)
                    yo = Fr()
                    f.op('act', lambda bk=bk, yo=yo: S_.copy(out=yo[:, 0:512], in_=bk[:, :]), reads=[bk], writes=[yo])
                    f.dma('sp', out_d[s_loc, r0:r0 + 128, half * 512:(half + 1) * 512], yo[:, 0:512], reads=[yo], writes=[outb])
    f.finish()
    f.schedule()
    f.emit()
    print('sched makespan us', f.makespan / 1e3)
    print('build: instr counts', f.cnt, 'waits', f.nwait, 'sbuf left', nc.sbuf_bytes_remaining)
    return nc


_NC_CACHE = {}


def kernel(**inputs):
    n = 8
    x = np.ascontiguousarray(inputs['x'], dtype=np.float32)
    if 'full' not in _NC_CACHE:
        _NC_CACHE['full'] = build()
    nc = _NC_CACHE['full']
    in_maps = []
    for c in range(n):
        m = {'x': np.ascontiguousarray(x[2 * c:2 * c + 2])}
        for nm in PARAM_NAMES:
            m[nm] = np.ascontiguousarray(inputs[nm], dtype=np.float32)
        in_maps.append(m)
    res = run_bass_kernel_spmd(nc, in_maps, core_ids=list(range(n)))
    out = np.concatenate([np.asarray(r['out'], dtype=np.float32) for r in res.results], axis=0)
    return out
```

```python
import types
import numpy as np
import concourse.bass as bass
import concourse.mybir as mybir
from concourse.bass_utils import run_bass_kernel_spmd

F32 = mybir.dt.float32
BF16 = mybir.dt.bfloat16
AF = mybir.ActivationFunctionType
ALU = mybir.AluOpType

D = 1024
KC = 8
SEQ = 2048
DEPTH = 2
INC = 2564
DFF = 2816
NFC = 22
EPS = 1e-6
T = 512
PASS_TOK = 1024
NT = PASS_TOK // T
NB = T // 128
FGROUPS = [(0, 3), (3, 3), (6, 3), (9, 3), (12, 3), (15, 3), (18, 2), (20, 2)]
SLOT_ELEMS = 10272
NSLOT = 4
ASPLIT = (0, 1152, 2304, 2564)
POOL_WINS = (2, 4, 8, 16)
POOL_CONV_Q = ()
PRIO_RANK = True
DMA_MARGIN_NS = 15000.0
XLAT_NS = 600.0

PARAM_NAMES = ['norm_mix_g', 'w_in', 'pool_w', 'pool_b', 'pool_scale', 'sgu_ln_g', 'sgu_ln_b', 'sgu_w', 'sgu_b',
               'sconv_w', 'ssd_conv_w', 'ssd_conv_b', 'ssd_dt_bias', 'ssd_a_log', 'ssd_d', 'ssd_norm_g',
               'w_out', 'norm_ffn_g', 'w_gate', 'w_up', 'w_down', 'final_norm_g']


class Buf:
    def __init__(self, t, name, dram=False):
        self.t = t
        self.name = name
        self.dram = dram
        self.w = None
        self.wg = []
        self.r = []

    def __getitem__(self, k):
        return self.t[k]


def _freeze(fn):
    if fn is None or fn.__closure__ is None:
        return fn
    cells = []
    for c in fn.__closure__:
        try:
            cells.append(types.CellType(c.cell_contents))
        except ValueError:
            cells.append(c)
    return types.FunctionType(fn.__code__, fn.__globals__, fn.__name__, fn.__defaults__, tuple(cells))


class Op:
    __slots__ = ('i', 'e', 'fn', 'kind', 'preds', 'succs', 'cost', 'tbl', 'lat', 'semkey', 'semval', 'start', 'fin', 'sbuf', 'nb')


class FW:
    def __init__(self, nc):
        self.nc = nc
        self.eng = {'pe': nc.tensor, 'act': nc.scalar, 'dve': nc.vector, 'pool': nc.gpsimd, 'sp': nc.sync}
        self.ops = []
        self.skip = False
        self.cnt = {k: 0 for k in self.eng}
        self.nwait = 0

    def sb(self, name, shape, dt=F32):
        return Buf(self.nc.alloc_sbuf_tensor(name, list(shape), dt), name)

    def ps(self, name, shape, dt=F32):
        return Buf(self.nc.alloc_psum_tensor(name, list(shape), dt), name)

    def _record(self, e, fn, kind, reads, writes, cost, tbl, lat=0.0, sbuf=None, nb=0):
        o = Op()
        o.i = len(self.ops); o.e = e; o.fn = _freeze(fn); o.kind = kind; o.cost = cost; o.tbl = tbl; o.lat = lat
        o.sbuf = sbuf; o.nb = nb; o.succs = []; o.semkey = None; o.semval = 0; o.start = 0.0; o.fin = 0.0
        preds = set()
        reads = [b for b in reads if not b.dram]
        writes = [b for b in writes if not b.dram]
        grouped = set()
        for b in reads:
            if b.w is not None:
                preds.add(b.w)
                preds.update(b.wg)
        for b in writes:
            if b.w is not None:
                if kind == 'dma' and self.ops[b.w].kind == 'dma' and self.ops[b.w].sbuf is sbuf and b is sbuf and not b.r:
                    grouped.add(id(b))
                    for w0 in b.wg[:1]:
                        preds.update(self.ops[w0].preds)
                else:
                    preds.add(b.w)
                    preds.update(b.wg)
            for r in b.r:
                preds.add(r)
        preds.discard(o.i)
        o.preds = sorted(preds)
        self.ops.append(o)
        for b in reads:
            b.r.append(o.i)
        for b in writes:
            if id(b) in grouped:
                b.wg.append(o.i)
            else:
                b.wg = [o.i] if kind == 'dma' else []
            b.w = o.i
            b.r = []
        return o

    def op(self, e, ins_fn, reads=(), writes=(), n=512, tbl=None):
        if self.skip:
            return None
        if e == 'act' and tbl is None:
            nm = ins_fn.__code__.co_names
            tbl = 'S' if 'Silu' in nm else ('E' if ('Exp' in nm or 'Ln' in nm) else None)
        if e == 'pe':
            cost = max(n, 64) / 2.4 + 12.0
        elif e == 'act':
            cost = n * 0.84 + 200.0
        elif e == 'dve':
            cost = n * 1.05 + 110.0
        else:
            cost = n * 2.0 + 200.0
        return self._record(e, ins_fn, 'c', reads, writes, cost, tbl)

    def dma(self, q, out_ap, in_ap, reads=(), writes=(), nbytes=262144, **kw):
        if self.skip:
            return None
        sb_side = None
        for b in list(writes) + list(reads):
            if not b.dram:
                sb_side = b
                break
        fn = (lambda: self.eng[q].dma_start(out=out_ap, in_=in_ap, **kw))
        issue = 1500.0 if q == 'pool' else 150.0
        is_store = any(b.dram for b in writes)
        return self._record(q, fn, 'dma', reads, writes, issue, None, lat=2500.0 + nbytes / 150.0, sbuf=sb_side, nb=(-1 if is_store else 0))

    def schedule(self):
        ops = self.ops
        N = len(ops)
        indeg = [len(o.preds) for o in ops]
        for o in ops:
            for p in o.preds:
                ops[p].succs.append(o.i)
        rank = [0.0] * N
        for o in reversed(ops):
            m = 0.0
            for s_ in o.succs:
                if rank[s_] > m:
                    m = rank[s_]
            rank[o.i] = o.cost + o.lat + m
        avail = {k: [] for k in self.eng}
        free = {k: 0.0 for k in self.eng}
        cur_tbl = [None]
        order = {k: [] for k in self.eng}
        for o in ops:
            if indeg[o.i] == 0:
                avail[o.e].append((0.0, o.i))
        done = 0
        LOOK = 400
        minidx = {k: 0 for k in self.eng}
        while done < N:
            best = None
            for k, lst in avail.items():
                if not lst:
                    continue
                fr = free[k]
                mr = min(lst)[0]
                t = fr if fr > mr else mr
                ci = None
                if PRIO_RANK:
                    br = -1.0
                    for (rt, idx) in lst:
                        if rt <= t and (rank[idx] > br):
                            br = rank[idx]; ci = idx
                else:
                    ck = None
                    for (rt, idx) in lst:
                        if rt <= t:
                            kk = (0 if ops[idx].kind == 'dma' else 1, idx)
                            if ck is None or kk < ck:
                                ck = kk; ci = idx
                if best is None or t < best[0] or (t == best[0] and ci < best[2]):
                    best = (t, k, ci)
            t, k, ci = best
            o = ops[ci]
            lst = avail[k]
            for j, (rt, idx) in enumerate(lst):
                if idx == ci:
                    lst.pop(j)
                    break
            c = o.cost
            if k == 'act' and o.tbl is not None and o.tbl != cur_tbl[0]:
                c += 2700.0
                cur_tbl[0] = o.tbl
            o.start = t
            free[k] = t + c
            o.fin = t + c + o.lat
            order[k].append(ci)
            done += 1
            for s_ in o.succs:
                indeg[s_] -= 1
                if indeg[s_] == 0:
                    so = ops[s_]
                    rt = max(ops[p].fin + (0.0 if ops[p].e == so.e else XLAT_NS) for p in so.preds)
                    if so.kind == 'dma' and so.e == 'pool' and rt > 0.0:
                        rt += DMA_MARGIN_NS
                    avail[so.e].append((rt, s_))
        self.order = order
        self.makespan = max(o.fin for o in ops)

    def emit(self):
        nc = self.nc
        ops = self.ops
        sem = {k: nc.alloc_semaphore('s_' + k) for k in self.eng}
        dsem = {}
        dcnt = {}
        for k, lst in self.order.items():
            c = 0
            for ci in lst:
                o = ops[ci]
                if o.kind == 'dma':
                    key = id(o.sbuf)
                    if key not in dsem:
                        dsem[key] = nc.alloc_semaphore('d%d_%s' % (len(dsem), o.sbuf.name))
                        dcnt[key] = 0
                    dcnt[key] += 16
                    o.semkey = ('d', key); o.semval = dcnt[key]
                else:
                    c += 1
                    o.semkey = ('e', k); o.semval = c
            self.cnt[k] = c
        for k, lst in self.order.items():
            eng = self.eng[k]
            seen = {}
            for ci in lst:
                o = ops[ci]
                need = {}
                for p in o.preds:
                    po = ops[p]
                    if po.semkey is None:
                        continue
                    if k == 'pe' and po.semkey == ('e', 'pe'):
                        continue
                    if need.get(po.semkey, 0) < po.semval:
                        need[po.semkey] = po.semval
                todo = [(key, v) for key, v in need.items() if seen.get(key, 0) < v]
                attach = None
                if todo and o.kind != 'wait':
                    attach = todo.pop()
                for key, v in todo:
                    sh = sem[key[1]] if key[0] == 'e' else dsem[key[1]]
                    eng.wait_ge(sh, v)
                    self.nwait += 1
                    seen[key] = v
                if o.kind == 'wait':
                    continue
                ins = o.fn()
                if attach is not None:
                    key, v = attach
                    ins._wait_ge(sem[key[1]] if key[0] == 'e' else dsem[key[1]], v)
                    seen[key] = v
                if o.kind == 'dma':
                    ins.then_inc(dsem[o.semkey[1]], 16)
                else:
                    ins.then_inc(sem[k], 1)

    def finish(self, e='sp'):
        o = Op()
        o.i = len(self.ops); o.e = e; o.fn = None; o.kind = 'wait'; o.cost = 10.0; o.tbl = None; o.lat = 0.0
        o.sbuf = None; o.nb = 0; o.succs = []; o.semkey = None; o.semval = 0; o.start = 0.0; o.fin = 0.0
        o.preds = [x.i for x in self.ops if x.kind == 'dma' and x.nb == -1]
        self.ops.append(o)
        return o


def build(npass=4, nlayers=DEPTH, final=True, dbg=None, stage=99):
    nc = bass.Bass("TRN2", target_bir_lowering=False)
    f = FW(nc)
    V, S_, P_ = nc.vector, nc.scalar, nc.tensor

    shapes = {
        'x': [2, SEQ, D], 'norm_mix_g': [DEPTH, D], 'w_in': [DEPTH, D, INC], 'pool_w': [DEPTH, 4, 64, 64],
        'pool_b': [DEPTH, 256], 'pool_scale': [DEPTH, 256], 'sgu_ln_g': [DEPTH, 256], 'sgu_ln_b': [DEPTH, 256],
        'sgu_w': [DEPTH, 4, 128, 128], 'sgu_b': [DEPTH, 4, 128], 'sconv_w': [DEPTH, 3, 256],
        'ssd_conv_w': [DEPTH, 4, 768], 'ssd_conv_b': [DEPTH, 768], 'ssd_dt_bias': [DEPTH, 4], 'ssd_a_log': [DEPTH, 4],
        'ssd_d': [DEPTH, 4], 'ssd_norm_g': [DEPTH, 256], 'w_out': [DEPTH, D, D], 'norm_ffn_g': [DEPTH, D],
        'w_gate': [DEPTH, D, DFF], 'w_up': [DEPTH, D, DFF], 'w_down': [DEPTH, DFF, D], 'final_norm_g': [D],
    }
    dr = {}
    for n, s in shapes.items():
        dr[n] = nc.dram_tensor(n, s, F32, kind="ExternalInput").ap()
    out_d = nc.dram_tensor("out", [2, SEQ, D], F32, kind="ExternalOutput").ap()
    wdram = Buf(None, 'wdram', dram=True)
    outb = Buf(None, 'outb', dram=True)
    dbg_aps = {}
    if dbg:
        for n, s in dbg.items():
            dbg_aps[n] = nc.dram_tensor('dbg_' + n, s, F32, kind="ExternalOutput").ap()
    dbgb = Buf(None, 'dbgb', dram=True)

    ones_f = f.sb('ones_f', [128, 128]); ones_b = f.sb('ones_b', [128, 128], BF16)
    ident_f = f.sb('ident_f', [128, 128]); ident_b = f.sb('ident_b', [128, 128], BF16)
    lmask = f.sb('lmask', [128, 128]); umask = f.sb('umask', [128, 128])
    negm = f.sb('negm', [128, 512], BF16)
    invwin = f.sb('invwin', [128, 2]); invcnt = f.sb('invcnt', [128, 2, 16])
    G = nc.gpsimd
    f.op('pool', lambda: G.memset(ones_f[:, :], 1.0), writes=[ones_f])
    f.op('pool', lambda: G.memset(ones_b[:, :], 1.0), writes=[ones_b])
    f.op('pool', lambda: G.memset(ident_f[:, :], 1.0), writes=[ident_f])
    f.op('pool', lambda: G.affine_select(out=ident_f[:, :], in_=ident_f[:, :], pattern=[[-1, 128]],
                                         compare_op=ALU.is_equal, fill=0.0, base=0, channel_multiplier=1),
         reads=[ident_f], writes=[ident_f])
    f.op('pool', lambda: G.tensor_copy(out=ident_b[:, :], in_=ident_f[:, :]), reads=[ident_f], writes=[ident_b])
    f.op('pool', lambda: G.memset(lmask[:, :], 1.0), writes=[lmask])
    f.op('pool', lambda: G.affine_select(out=lmask[:, :], in_=lmask[:, :], pattern=[[1, 128]],
                                         compare_op=ALU.is_ge, fill=0.0, base=0, channel_multiplier=-1),
         reads=[lmask], writes=[lmask])
    f.op('pool', lambda: G.memset(umask[:, :], 1.0), writes=[umask])
    f.op('pool', lambda: G.affine_select(out=umask[:, :], in_=umask[:, :], pattern=[[-1, 128]],
                                         compare_op=ALU.is_gt, fill=0.0, base=0, channel_multiplier=1),
         reads=[umask], writes=[umask])
    for hd_ in range(4):
        f.op('pool', lambda hd_=hd_: G.tensor_scalar(out=negm[:, hd_ * 128:(hd_ + 1) * 128], in0=umask[:, :], scalar1=-16384.0, scalar2=None, op0=ALU.mult),
             reads=[umask], writes=[negm])
    for j in range(2):
        for hf in range(2):
            win = POOL_WINS[2 * j + hf]
            sl = slice(hf * 64, hf * 64 + 64)
            f.op('pool', lambda j=j, sl=sl, win=win: G.memset(invwin[sl, j:j + 1], 1.0 / win), writes=[invwin])
            f.op('pool', lambda j=j, sl=sl, win=win: G.memset(invcnt[sl, j, :], 1.0 / win), writes=[invcnt])
            for t in range(win - 1):
                f.op('pool', lambda j=j, sl=sl, t=t: G.memset(invcnt[sl, j, t:t + 1], 1.0 / (t + 1)), writes=[invcnt])

    LP = []
    for l in range(DEPTH):
        p = {}
        p['g1'] = f.sb('g1_%d' % l, [128, KC]); p['g2'] = f.sb('g2_%d' % l, [128, KC])
        p['pwb'] = f.sb('pwb_%d' % l, [128, 2, 128], BF16)
        p['pb'] = f.sb('pb_%d' % l, [128, 2]); p['psc'] = f.sb('psc_%d' % l, [128, 2])
        p['lng'] = f.sb('lng_%d' % l, [128, 256]); p['lnb'] = f.sb('lnb_%d' % l, [128, 256])
        p['wst'] = f.sb('wst_%d' % l, [128, 4, 128], BF16)
        p['bsb'] = f.sb('bsb_%d' % l, [128, 2, 128])
        p['scw'] = f.sb('scw_%d' % l, [128, 2, 3]); p['cw'] = f.sb('cw_%d' % l, [128, 6, 4]); p['cb'] = f.sb('cb_%d' % l, [128, 6])
        p['dtb'] = f.sb('dtb_%d' % l, [128, 4]); p['ab'] = f.sb('ab_%d' % l, [128, 4])
        p['dsk'] = f.sb('dsk_%d' % l, [128, 2]); p['ng'] = f.sb('ng_%d' % l, [128, 2])
        p['pool_halo'] = f.sb('phalo_%d' % l, [128, 2, 16])
        p['sc_halo'] = f.sb('schalo_%d' % l, [128, 2, 2])
        p['ssd_halo'] = f.sb('sshalo_%d' % l, [128, 6, 3])
        p['prev_f'] = f.sb('prevf_%d' % l, [128, 256])
        p['prev_b'] = f.sb('prevb_%d' % l, [128, 256], BF16)
        LP.append(p)
    gfin = f.sb('gfin', [128, KC])

    hT = [f.sb('h_%d' % t, [128, KC, T]) for t in range(NT)]
    hn2 = [f.sb('hn2_%d' % t, [128, KC, T], BF16) for t in range(NT)]
    mxs = [f.sb('mix_%d' % i, [128, KC, T], BF16) for i in range(2)]
    mxks = [[Buf(None, 'mxk_%d_%d' % (i, k)) for k in range(KC)] for i in range(2)]
    hn2k = [[Buf(None, 'hn2k_%d_%d' % (t, k)) for k in range(KC)] for t in range(NT)]
    hk = [[Buf(None, 'hk_%d_%d' % (t, k)) for k in range(KC)] for t in range(NT)]
    xk = [Buf(None, 'xk_%d' % q) for q in range(6)]
    zk = [Buf(None, 'zk_%d' % j) for j in range(2)]
    uk = [Buf(None, 'uk_%d' % j) for j in range(2)]
    yk = [Buf(None, 'yk_%d' % j) for j in range(2)]
    fk = [[Buf(None, 'fk_%d_%d' % (i, j)) for j in range(3)] for i in range(2)]
    slots = [f.sb('slot_%d' % i, [128, SLOT_ELEMS], BF16) for i in range(NSLOT)]
    banks = [f.ps('bank_%d' % i, [128, 512]) for i in range(8)]
    u_sb = f.sb('u_sb', [128, 2, T], BF16)
    zs = f.sb('zs', [128, 2, T], BF16)
    ytile = f.sb('ytile', [128, 2, T], F32)
    xbc = f.sb('xbc', [128, 6, T], BF16)
    ffacts = [f.sb('ffact_%d' % i, [128, 3, T], BF16) for i in range(2)]
    fring = [f.sb('fr_%d' % i, [128, 528], F32) for i in range(9)]
    bring = [f.sb('br_%d' % i, [128, 512], BF16) for i in range(7)]
    st = {'bank': 0, 'f': 0, 'b': 0}

    BANK_POOLS = {'in': (0, 1, 2, 3, 4, 5, 6, 7), 'ssd': (0, 1, 2, 3, 4, 5, 6, 7), 'misc': (0, 1, 2, 3, 4, 5, 6, 7), 'gu': (0, 1, 2), 'dn': (3, 4, 5, 6, 7)}
    bank_ctr = {k: 0 for k in BANK_POOLS}

    def bank(kind='in'):
        pool_ = BANK_POOLS[kind]
        b = banks[pool_[bank_ctr[kind] % len(pool_)]]
        bank_ctr[kind] += 1
        return b

    def Fr():
        b = fring[st['f'] % len(fring)]
        st['f'] += 1
        return b

    def Br():
        b = bring[st['b'] % len(bring)]
        st['b'] += 1
        return b

    rings = {}

    def ring(name, shape, dt=F32, n=2):
        if name not in rings:
            rings[name] = [[f.sb('%s_%d' % (name, i), shape, dt) for i in range(n)], 0]
        r = rings[name]
        b = r[0][r[1] % n]
        r[1] += 1
        return b

    def v4(buf):
        return buf[:, 0:512].rearrange("p (h l) -> p h l", h=4)

    stg = Fr()
    f.op('dve', lambda: V.memset(stg[:, 0:128], 0.0), writes=[stg])

    def rows(dst_r0, src_ap_2d, n):
        f.dma('sp', stg[dst_r0:dst_r0 + n, 0:128], src_ap_2d, writes=[stg])
    for l in range(DEPTH):
        base = 58 * l
        rows(base + 0, dr['norm_mix_g'][l].rearrange("(k p) -> k p", p=128), 8)
        rows(base + 8, dr['norm_ffn_g'][l].rearrange("(k p) -> k p", p=128), 8)
        rows(base + 16, dr['pool_b'][l].rearrange("(k p) -> k p", p=128), 2)
        rows(base + 18, dr['pool_scale'][l].rearrange("(k p) -> k p", p=128), 2)
        rows(base + 20, dr['sconv_w'][l].rearrange("k (j p) -> (k j) p", p=128), 6)
        rows(base + 26, dr['ssd_conv_w'][l].rearrange("k (q p) -> (k q) p", p=128), 24)
        rows(base + 50, dr['ssd_conv_b'][l].rearrange("(q p) -> q p", p=128), 6)
        rows(base + 56, dr['ssd_norm_g'][l].rearrange("(j p) -> j p", p=128), 2)
    rows(116, dr['final_norm_g'].rearrange("(k p) -> k p", p=128), 8)
    bkp = bank()
    f.op('pe', lambda: P_.transpose(bkp[:, 0:128], stg[:, 0:128], ident_f[:, :]), reads=[stg, ident_f], writes=[bkp], n=512)
    for l in range(DEPTH):
        p = LP[l]
        base = 58 * l
        cps = [(p['g1'][:, :], bkp[:, base:base + 8], p['g1']), (p['g2'][:, :], bkp[:, base + 8:base + 16], p['g2']),
               (p['pb'][:, :], bkp[:, base + 16:base + 18], p['pb']), (p['psc'][:, :], bkp[:, base + 18:base + 20], p['psc']),
               (p['scw'][:, :, :], bkp[:, base + 20:base + 26].rearrange("p (k j) -> p j k", k=3), p['scw']),
               (p['cw'][:, :, :], bkp[:, base + 26:base + 50].rearrange("p (k q) -> p q k", k=4), p['cw']),
               (p['cb'][:, :], bkp[:, base + 50:base + 56], p['cb']), (p['ng'][:, :], bkp[:, base + 56:base + 58], p['ng'])]
        for (o, i, bb) in cps:
            f.op('dve', lambda o=o, i=i: V.tensor_copy(out=o, in_=i), reads=[bkp], writes=[bb])
    f.op('dve', lambda: V.tensor_copy(out=gfin[:, :], in_=bkp[:, 116:124]), reads=[bkp], writes=[gfin])

    for l in range(DEPTH):
        p = LP[l]
        rA = Fr(); rB = Fr(); rC = Fr()
        f.dma('sp', rA[0:1, 0:256], dr['sgu_ln_g'][l:l + 1, :], writes=[rA])
        f.dma('sp', rA[0:1, 256:512], dr['sgu_ln_b'][l:l + 1, :], writes=[rA])
        f.dma('sp', rB[0:1, 0:512], dr['sgu_b'][l:l + 1].rearrange("o h i -> o (h i)"), writes=[rB])
        f.dma('sp', rC[0:1, 0:4], dr['ssd_dt_bias'][l:l + 1, :], writes=[rC])
        f.dma('sp', rC[0:1, 4:8], dr['ssd_a_log'][l:l + 1, :], writes=[rC])
        f.dma('sp', rC[0:1, 8:12], dr['ssd_d'][l:l + 1, :], writes=[rC])
        bA = bank(); bB = bank(); bC = bank()
        f.op('pe', lambda bA=bA, rA=rA: P_.matmul(bA[:, :], lhsT=ones_f[0:1, :], rhs=rA[0:1, 0:512], start=True, stop=True), reads=[ones_f, rA], writes=[bA])
        f.op('pe', lambda bB=bB, rB=rB: P_.matmul(bB[:, :], lhsT=ones_f[0:1, :], rhs=rB[0:1, 0:512], start=True, stop=True), reads=[ones_f, rB], writes=[bB])
        f.op('pe', lambda bC=bC, rC=rC: P_.matmul(bC[:, 0:12], lhsT=ones_f[0:1, :], rhs=rC[0:1, 0:12], start=True, stop=True), reads=[ones_f, rC], writes=[bC])
        f.op('dve', lambda p=p, bA=bA: V.tensor_copy(out=p['lng'][:, :], in_=bA[:, 0:256]), reads=[bA], writes=[p['lng']], n=256)
        f.op('dve', lambda p=p, bA=bA: V.tensor_copy(out=p['lnb'][:, :], in_=bA[:, 256:512]), reads=[bA], writes=[p['lnb']])
        for jj in range(2):
            for hf in range(2):
                hh = 2 * jj + hf
                f.op('dve', lambda p=p, bB=bB, jj=jj, hf=hf, hh=hh: V.tensor_copy(out=p['bsb'][hf * 64:hf * 64 + 64, jj, :], in_=bB[hf * 64:hf * 64 + 64, hh * 128:(hh + 1) * 128]),
                     reads=[bB], writes=[p['bsb']], n=256)
                f.op('dve', lambda p=p, bC=bC, jj=jj, hf=hf, hh=hh: V.tensor_copy(out=p['dsk'][hf * 64:hf * 64 + 64, jj:jj + 1], in_=bC[hf * 64:hf * 64 + 64, 8 + hh:9 + hh]),
                     reads=[bC], writes=[p['dsk']], n=16)
        f.op('dve', lambda p=p, bC=bC: V.tensor_copy(out=p['dtb'][:, :], in_=bC[:, 0:4]), reads=[bC], writes=[p['dtb']], n=16)
        f.op('dve', lambda p=p, bC=bC: V.tensor_copy(out=p['ab'][:, :], in_=bC[:, 4:8]), reads=[bC], writes=[p['ab']], n=16)
        f.op('act', lambda p=p: S_.activation(out=p['ab'][:, :], in_=p['ab'][:, :], func=AF.Exp), reads=[p['ab']], writes=[p['ab']], n=16)
        f.op('act', lambda p=p: S_.mul(out=p['ab'][:, :], in_=p['ab'][:, :], mul=-1.0), reads=[p['ab']], writes=[p['ab']], n=16)
        pwf = Fr()
        f.op('dve', lambda pwf=pwf: V.memset(pwf[:, 0:256], 0.0), writes=[pwf], n=256)
        for j in range(2):
            for hf in range(2):
                f.dma('sp', pwf[hf * 64:hf * 64 + 64, j * 128 + hf * 64:j * 128 + hf * 64 + 64], dr['pool_w'][l, 2 * j + hf], writes=[pwf])
        f.op('dve', lambda p=p, pwf=pwf: V.tensor_copy(out=p['pwb'][:, :, :], in_=pwf[:, 0:256].rearrange("p (j d) -> p j d", j=2)), reads=[pwf], writes=[p['pwb']], n=256)
        wsf = Fr()
        for hd in range(4):
            f.dma('sp', wsf[:, hd * 128:(hd + 1) * 128], dr['sgu_w'][l, hd], writes=[wsf])
        bk = bank()
        for hd in range(4):
            f.op('pe', lambda hd=hd, bk=bk, wsf=wsf: P_.transpose(bk[:, hd * 128:(hd + 1) * 128], wsf[:, hd * 128:(hd + 1) * 128], ident_f[:, :]),
                 reads=[wsf, ident_f], writes=[bk], n=512)
        f.op('dve', lambda bk=bk, p=p: V.tensor_copy(out=p['wst'][:, :, :], in_=bk[:, :].rearrange("p (h i) -> p h i", h=4)),
             reads=[bk], writes=[p['wst']])
        f.op('dve', lambda p=p: V.memset(p['wst'][64:128, :, 0:64], 0.0), reads=[], writes=[p['wst']])

    win_v = [dr['w_in'][l].rearrange("(k p) c -> p k c", p=128) for l in range(DEPTH)]
    wout_v = [dr['w_out'][l].rearrange("(k p) c -> p k c", p=128) for l in range(DEPTH)]
    wg_v = [dr['w_gate'][l].rearrange("(k p) c -> p k c", p=128) for l in range(DEPTH)]
    wu_v = [dr['w_up'][l].rearrange("(k p) c -> p k c", p=128) for l in range(DEPTH)]
    wd_v = [dr['w_down'][l].rearrange("(j p) c -> p j c", p=128) for l in range(DEPTH)]

    def view(slot, off, k, c):
        return slot.t[:, off:off + k * c].rearrange("p (k c) -> p k c", k=k)

    item_state = {'n': 0}

    def load_item(l, kind):
        slot = slots[item_state['n'] % NSLOT]
        item_state['n'] += 1
        it = {'slot': slot}
        if kind in ('A1', 'A2', 'A3'):
            ai = int(kind[1]) - 1
            c0, c1 = ASPLIT[ai], ASPLIT[ai + 1]
            v = view(slot, 0, KC, c1 - c0)
            f.dma('pool', v[:, :, :], win_v[l][:, :, c0:c1], writes=[slot], nbytes=KC * (c1 - c0) * 4 * 128)
            it['w'] = v; it['c0'] = c0; it['c1'] = c1
            if kind == 'A3':
                vo = view(slot, KC * (c1 - c0), KC, D)
                f.dma('pool', vo[:, :, :], wout_v[l][:, :, :], writes=[slot], nbytes=KC * D * 4 * 128)
                it['wo'] = vo
        else:
            c0, gcount = FGROUPS[kind]
            gw = gcount * 128
            vg = view(slot, 0, KC, gw)
            vu = view(slot, KC * gw, KC, gw)
            vd = view(slot, 2 * KC * gw, gcount, D)
            f.dma('pool', vg[:, :, :], wg_v[l][:, :, c0 * 128:c0 * 128 + gw], writes=[slot], nbytes=KC * gw * 4 * 128)
            f.dma('pool', vu[:, :, :], wu_v[l][:, :, c0 * 128:c0 * 128 + gw], writes=[slot], nbytes=KC * gw * 4 * 128)
            f.dma('pool', vd[:, :, :], wd_v[l][:, c0:c0 + gcount, :], writes=[slot], nbytes=gcount * D * 4 * 128)
            it['wg'] = vg; it['wu'] = vu; it['wd'] = vd; it['G'] = gcount
        return it

    sched = []
    for ps_i in range(npass):
        for l in range(nlayers):
            sched += [(l, 'A1'), (l, 'A2'), (l, 'A3')]
            for gi in range(len(FGROUPS)):
                sched.append((l, gi))
    loaded = []

    def ensure_loaded(upto):
        while len(loaded) <= upto and len(loaded) < len(sched):
            loaded.append(load_item(*sched[len(loaded)]))

    def rmsnorm_tile(src, dst, gvec, dstk):
        srck = hk[hT.index(src)]
        acc = bank('misc')
        for k in range(KC):
            sq = Br()
            if k % 4 == 3:
                f.op('dve', lambda k=k, sq=sq: V.tensor_tensor(out=sq[:, :], in0=src[:, k, :], in1=src[:, k, :], op=ALU.mult), reads=[srck[k]], writes=[sq])
            else:
                f.op('act', lambda k=k, sq=sq: S_.activation(out=sq[:, :], in_=src[:, k, :], func=AF.Square), reads=[srck[k]], writes=[sq])
            f.op('pe', lambda k=k, sq=sq: P_.matmul(acc[:, :], lhsT=ones_b[:, :], rhs=sq[:, :], start=(k == 0), stop=(k == KC - 1)),
                 reads=[ones_b, sq], writes=[acc])
        rs = Fr()
        f.op('act', lambda: S_.activation(out=rs[:, 0:T], in_=acc[:, :], func=AF.Ln, scale=1.0 / D, bias=EPS), reads=[acc], writes=[rs])
        f.op('act', lambda: S_.activation(out=rs[:, 0:T], in_=rs[:, 0:T], func=AF.Exp, scale=-0.5), reads=[rs], writes=[rs])
        for k in range(KC):
            f.op('dve', lambda k=k: V.scalar_tensor_tensor(out=dst[:, k, :], in0=src[:, k, :], scalar=gvec[:, k:k + 1], in1=rs[:, 0:T],
                                                           op0=ALU.mult, op1=ALU.mult), reads=[srck[k], gvec, rs], writes=[dstk[k]])

    def dump(name, src_ap, srcbuf):
        if name in dbg_aps:
            f.dma('sp', dbg_aps[name], src_ap, reads=[srcbuf], writes=[dbgb])

    item_idx = 0
    ensure_loaded(NSLOT - 1)
    for ps_i in range(npass):
        s_loc = ps_i // 2
        hp = ps_i % 2
        tok0 = hp * PASS_TOK
        for t in range(NT):
            for b in range(NB):
                r0 = tok0 + t * T + b * 128
                for half in range(2):
                    xin = Fr()
                    f.dma('sp', xin[:, 0:512], dr['x'][s_loc, r0:r0 + 128, half * 512:(half + 1) * 512], writes=[xin])
                    bk = bank()
                    for kk in range(4):
                        f.op('pe', lambda kk=kk, bk=bk, xin=xin: P_.transpose(bk[:, kk * 128:(kk + 1) * 128], xin[:, kk * 128:(kk + 1) * 128], ident_f[:, :]),
                             reads=[xin, ident_f], writes=[bk], n=512)
                    f.op('act', lambda half=half, bk=bk, t=t, b=b: S_.copy(out=hT[t][:, half * 4:half * 4 + 4, b * 128:(b + 1) * 128],
                                                                         in_=bk[:, :].rearrange("p (k c) -> p k c", k=4)),
                         reads=[bk], writes=[hk[t][half * 4 + i_] for i_ in range(4)])
        if hp == 0:
            for l in range(nlayers):
                p = LP[l]
                for nm in ('pool_halo', 'sc_halo', 'ssd_halo', 'prev_f', 'prev_b'):
                    buf = p[nm]
                    if len(buf.t.shape) == 3:
                        f.op('dve', lambda buf=buf: V.memset(buf[:, :, :], 0.0), writes=[buf])
                    else:
                        f.op('dve', lambda buf=buf: V.memset(buf[:, :], 0.0), writes=[buf])

        for l in range(nlayers):
            p = LP[l]
            itA = [loaded[item_idx], loaded[item_idx + 1], loaded[item_idx + 2]]
            WO, sWO = itA[2]['wo'], itA[2]['slot']

            def proj_fm(dst_bank, c0, src, n=128):
                srck = hn2k[hn2.index(src)]
                a = c0
                while a < c0 + n:
                    it = [i for i in itA if i['c0'] <= a < i['c1']][0]
                    e = min(c0 + n, it['c1'])
                    if (a - c0) % 128 != 0:
                        lim = (a - c0) & -(a - c0)
                        e = min(e, a + lim)
                    o0, o1 = a - c0, e - c0
                    for k in range(KC):
                        f.op('pe', lambda k=k, it=it, a=a, e=e, o0=o0, o1=o1: P_.matmul(dst_bank[o0:o1, :], lhsT=it['w'][:, k, a - it['c0']:e - it['c0']], rhs=src[:, k, :],
                                                                                      start=(k == 0), stop=(k == KC - 1)), reads=[it['slot'], srck[k]], writes=[dst_bank])
                    a = e

            def proj_tm(dst_ap, dst_bank, c0, n, src, b):
                it = [i for i in itA if i['c0'] <= c0 and c0 + n <= i['c1']][0]
                for k in range(KC):
                    f.op('pe', lambda k=k, it=it: P_.matmul(dst_ap, lhsT=src[:, k, b * 128:(b + 1) * 128], rhs=it['w'][:, k, c0 - it['c0']:c0 + n - it['c0']],
                                                          start=(k == 0), stop=(k == KC - 1)), reads=[it['slot'], hn2k[hn2.index(src)][k]], writes=[dst_bank], n=n)

            for t in range(NT):
                h = hT[t]
                hnb = hn2[t]
                mx = mxs[t % 2]
                mxk = mxks[t % 2]
                first_tile = (hp == 0 and t == 0)
                f.skip = False
                if t == 0:
                    rmsnorm_tile(h, hnb, p['g1'], hn2k[t])
                    for t2 in range(1, NT):
                        rmsnorm_tile(hT[t2], hn2[t2], p['g1'], hn2k[t2])

                f.skip = stage < 2
                for j in range(2):
                    xe = Fr(); sa = Fr(); sbb = Fr()
                    f.op('act', lambda j=j, xe=xe: S_.copy(out=xe[:, 0:16], in_=p['pool_halo'][:, j, :]), reads=[p['pool_halo']], writes=[xe], n=16)
                    bk = bank()
                    proj_fm(bk, j * 128, hnb)
                    f.op('act', lambda bk=bk, xe=xe: S_.copy(out=xe[:, 16:528], in_=bk[:, :]), reads=[bk], writes=[xe])
                    f.op('act', lambda j=j, xe=xe: S_.copy(out=p['pool_halo'][:, j, :], in_=xe[:, 512:528]), reads=[xe], writes=[p['pool_halo']], n=16)
                    f.op('dve', lambda xe=xe, sa=sa: V.tensor_tensor(out=sa[:, 1:528], in0=xe[:, 1:528], in1=xe[:, 0:527], op=ALU.add), reads=[xe], writes=[sa])
                    f.op('dve', lambda sa=sa, sbb=sbb: V.tensor_tensor(out=sbb[:, 3:528], in0=sa[:, 3:528], in1=sa[:, 1:526], op=ALU.add), reads=[sa], writes=[sbb])
                    if j == 1:
                        f.op('dve', lambda sa=sa, sbb=sbb: V.tensor_tensor(out=sa[:, 7:528], in0=sbb[:, 7:528], in1=sbb[:, 3:524], op=ALU.add), reads=[sbb, sa], writes=[sa])
                        f.op('dve', lambda sa=sa, sbb=sbb: V.tensor_tensor(out=sbb[64:128, 15:528], in0=sa[64:128, 15:528], in1=sa[64:128, 7:520], op=ALU.add),
                             reads=[sa, sbb], writes=[sbb])
                    pooled = Br()
                    for hf in range(2):
                        src = sa if hf == 0 else sbb
                        sl = slice(hf * 64, hf * 64 + 64)
                        f.op('dve', lambda j=j, sl=sl, src=src, xe=xe, pooled=pooled: V.scalar_tensor_tensor(out=pooled[sl, :], in0=src[sl, 16:528], scalar=invwin[sl, j:j + 1],
                                                                                                            in1=xe[sl, 16:528], op0=ALU.mult, op1=ALU.subtract),
                             reads=[src, xe, invwin], writes=[pooled])
                        if first_tile:
                            tmpc = ring('tmpc', [128, 16], F32, 2)
                            f.op('dve', lambda j=j, sl=sl, src=src, tmpc=tmpc: V.tensor_tensor(out=tmpc[sl, :], in0=src[sl, 16:32], in1=invcnt[sl, j, :], op=ALU.mult),
                                 reads=[src, invcnt], writes=[tmpc])
                            f.op('dve', lambda sl=sl, tmpc=tmpc, xe=xe, pooled=pooled: V.tensor_tensor(out=pooled[sl, 0:16], in0=tmpc[sl, :], in1=xe[sl, 16:32], op=ALU.subtract),
                                 reads=[tmpc, xe, pooled], writes=[pooled])
                    bk = bank()
                    f.op('pe', lambda j=j, bk=bk, pooled=pooled: P_.matmul(bk[:, :], lhsT=p['pwb'][:, j, :], rhs=pooled[:, :], start=True, stop=True),
                         reads=[p['pwb'], pooled], writes=[bk])
                    f.op('dve', lambda j=j, bk=bk: V.tensor_scalar(out=mx[:, j, :], in0=bk[:, :], scalar1=p['pb'][:, j:j + 1], scalar2=p['psc'][:, j:j + 1],
                                                                  op0=ALU.add, op1=ALU.mult), reads=[bk, p['pb'], p['psc']], writes=[mxk[j]])

                f.skip = stage < 3
                for j in range(2):
                    bk = bank()
                    proj_fm(bk, 256 + j * 128, hnb)
                    f.op('act', lambda j=j, bk=bk: S_.copy(out=u_sb[:, j, :], in_=bk[:, :]), reads=[bk], writes=[uk[j]])
                for b in range(NB):
                    bk = bank()
                    proj_tm(bk[:, 0:256], bk, 512, 256, hnb, b)
                    v_sb = Fr()
                    f.op('act', lambda bk=bk, v_sb=v_sb: S_.copy(out=v_sb[:, 0:256], in_=bk[:, 0:256]), reads=[bk], writes=[v_sb], n=256)
                    stt = ring('bnst', [128, 6], F32, 2)
                    mv = ring('bnmv', [128, 2], F32, 2)
                    f.op('dve', lambda stt=stt, v_sb=v_sb: V.bn_stats(out=stt[:, :], in_=v_sb[:, 0:256]), reads=[v_sb], writes=[stt], n=16)
                    f.op('dve', lambda stt=stt, mv=mv: V.bn_aggr(out=mv[:, :], in_=stt[:, :]), reads=[stt], writes=[mv], n=16)
                    rstd = ring('lnrstd', [128, 1], F32, 2)
                    f.op('act', lambda mv=mv, rstd=rstd: S_.activation(out=rstd[:, :], in_=mv[:, 1:2], func=AF.Ln, bias=EPS), reads=[mv], writes=[rstd], n=16)
                    f.op('act', lambda rstd=rstd: S_.activation(out=rstd[:, :], in_=rstd[:, :], func=AF.Exp, scale=-0.5), reads=[rstd], writes=[rstd], n=16)
                    vn = Fr()
                    f.op('dve', lambda v_sb=v_sb, mv=mv, rstd=rstd, vn=vn: V.tensor_scalar(out=vn[:, 0:256], in0=v_sb[:, 0:256], scalar1=mv[:, 0:1], scalar2=rstd[:, 0:1],
                                                                                        op0=ALU.subtract, op1=ALU.mult), reads=[v_sb, mv, rstd], writes=[vn], n=16)
                    f.op('dve', lambda vn=vn: V.tensor_tensor(out=vn[:, 0:256], in0=vn[:, 0:256], in1=p['lng'][:, :], op=ALU.mult), reads=[vn, p['lng']], writes=[vn], n=256)
                    vnb = Br()
                    f.op('dve', lambda vn=vn, vnb=vnb: V.tensor_tensor(out=vnb[:, 0:256], in0=vn[:, 0:256], in1=p['lnb'][:, :], op=ALU.add), reads=[vn, p['lnb']], writes=[vnb], n=256)
                    bk2 = bank()
                    for hd in range(4):
                        jj, hf = hd // 2, hd % 2
                        f.op('pe', lambda hd=hd, jj=jj, hf=hf, bk2=bk2, vnb=vnb: P_.matmul(bk2[hf * 64:hf * 64 + 64, jj * 128:(jj + 1) * 128],
                                                                                         lhsT=vnb[:, hd * 64:(hd + 1) * 64], rhs=p['wst'][:, hd, :], start=True, stop=True),
                             reads=[vnb, p['wst']], writes=[bk2], n=128)
                    sg = Fr()
                    f.op('dve', lambda bk2=bk2, sg=sg: V.tensor_tensor(out=sg[:, 0:256].rearrange("p (j i) -> p j i", j=2), in0=bk2[:, 0:256].rearrange("p (j i) -> p j i", j=2),
                                                                      in1=p['bsb'][:, :, :], op=ALU.add), reads=[bk2, p['bsb']], writes=[sg], n=256)
                    f.op('dve', lambda sg=sg, b=b: V.tensor_tensor(out=mx[:, 2:4, b * 128:(b + 1) * 128], in0=sg[:, 0:256].rearrange("p (j i) -> p j i", j=2),
                                                                  in1=u_sb[:, :, b * 128:(b + 1) * 128], op=ALU.mult), reads=[sg, uk[0], uk[1]], writes=[mxk[2], mxk[3]], n=256)

                f.skip = stage < 4
                for j in range(2):
                    pext = Fr()
                    f.op('act', lambda j=j, pext=pext: S_.copy(out=pext[:, 0:2], in_=p['sc_halo'][:, j, :]), reads=[p['sc_halo']], writes=[pext], n=16)
                    bk = bank()
                    proj_fm(bk, 1024 + j * 128, hnb)
                    cg = Fr()
                    f.op('act', lambda bk=bk, cg=cg: S_.copy(out=cg[:, 0:T], in_=bk[:, :]), reads=[bk], writes=[cg])
                    bk = bank()
                    proj_fm(bk, 1280 + j * 128, hnb)
                    f.op('dve', lambda bk=bk, cg=cg, pext=pext: V.tensor_tensor(out=pext[:, 2:514], in0=bk[:, :], in1=cg[:, 0:T], op=ALU.mult), reads=[bk, cg], writes=[pext])
                    f.op('act', lambda j=j, pext=pext: S_.copy(out=p['sc_halo'][:, j, :], in_=pext[:, 512:514]), reads=[pext], writes=[p['sc_halo']], n=16)
                    acc = Fr()
                    f.op('dve', lambda j=j, acc=acc, pext=pext: V.tensor_scalar(out=acc[:, 0:T], in0=pext[:, 2:514], scalar1=p['scw'][:, j, 2:3], scalar2=None, op0=ALU.mult),
                         reads=[pext, p['scw']], writes=[acc])
                    f.op('dve', lambda j=j, acc=acc, pext=pext: V.scalar_tensor_tensor(out=acc[:, 0:T], in0=pext[:, 1:513], scalar=p['scw'][:, j, 1:2], in1=acc[:, 0:T],
                                                                                      op0=ALU.mult, op1=ALU.add), reads=[pext, p['scw'], acc], writes=[acc])
                    f.op('dve', lambda j=j, acc=acc, pext=pext: V.scalar_tensor_tensor(out=acc[:, 0:T], in0=pext[:, 0:512], scalar=p['scw'][:, j, 0:1], in1=acc[:, 0:T],
                                                                                      op0=ALU.mult, op1=ALU.add), reads=[pext, p['scw'], acc], writes=[acc])
                    bk = bank()
                    proj_fm(bk, 768 + j * 128, hnb)
                    f.op('dve', lambda j=j, bk=bk, acc=acc: V.tensor_tensor(out=mx[:, 4 + j, :], in0=bk[:, :], in1=acc[:, 0:T], op=ALU.mult), reads=[bk, acc], writes=[mxk[4 + j]])

                f.skip = stage < 5.1
                for q in range(6):
                    bk = bank()
                    proj_fm(bk, 1792 + q * 128, hnb)
                    cin = Fr()
                    f.op('act', lambda q=q, cin=cin: S_.copy(out=cin[:, 0:3], in_=p['ssd_halo'][:, q, :]), reads=[p['ssd_halo']], writes=[cin], n=16)
                    f.op('act', lambda bk=bk, cin=cin: S_.copy(out=cin[:, 3:515], in_=bk[:, :]), reads=[bk], writes=[cin])
                    f.op('act', lambda q=q, cin=cin: S_.copy(out=p['ssd_halo'][:, q, :], in_=cin[:, 512:515]), reads=[cin], writes=[p['ssd_halo']], n=16)
                    acc = Fr()
                    ce, CE = ('pool', G) if q in POOL_CONV_Q else ('dve', V)
                    f.op(ce, lambda q=q, acc=acc, cin=cin, CE=CE: CE.tensor_scalar(out=acc[:, 0:T], in0=cin[:, 3:515], scalar1=p['cw'][:, q, 3:4], scalar2=None, op0=ALU.mult),
                         reads=[cin, p['cw']], writes=[acc])
                    for kk in (2, 1, 0):
                        f.op(ce, lambda q=q, kk=kk, acc=acc, cin=cin, CE=CE: CE.scalar_tensor_tensor(out=acc[:, 0:T], in0=cin[:, kk:kk + 512], scalar=p['cw'][:, q, kk:kk + 1], in1=acc[:, 0:T],
                                                                                                   op0=ALU.mult, op1=ALU.add), reads=[cin, p['cw'], acc], writes=[acc])
                    f.op('act', lambda q=q, acc=acc: S_.activation(out=xbc[:, q, :], in_=acc[:, 0:T], func=AF.Silu, bias=p['cb'][:, q:q + 1]),
                         reads=[acc, p['cb']], writes=[xk[q]])
                for j in range(2):
                    bk = bank()
                    proj_fm(bk, 1536 + j * 128, hnb)
                    f.op('act', lambda j=j, bk=bk: S_.activation(out=zs[:, j, :], in_=bk[:, :], func=AF.Silu), reads=[bk], writes=[zk[j]])
                f.skip = stage < 5.2
                bkd = bank()
                for b in range(NB):
                    proj_tm(bkd[:, b * 4:(b + 1) * 4], bkd, 2560, 4, hnb, b)
                delta = ring('delta', [128, NB, 4], F32, 1)
                da = ring('da', [128, NB, 4], F32, 1)
                for b in range(NB):
                    f.op('dve', lambda b=b: V.tensor_tensor(out=delta[:, b, :], in0=bkd[:, b * 4:(b + 1) * 4], in1=p['dtb'][:, :], op=ALU.add), reads=[bkd, p['dtb']], writes=[delta], n=16)
                f.op('act', lambda: S_.activation(out=delta[:, :, :], in_=delta[:, :, :], func=AF.Exp), reads=[delta], writes=[delta], n=16)
                f.op('act', lambda: S_.activation(out=delta[:, :, :], in_=delta[:, :, :], func=AF.Ln, bias=1.0), reads=[delta], writes=[delta], n=16)
                for b in range(NB):
                    f.op('dve', lambda b=b: V.tensor_tensor(out=da[:, b, :], in0=delta[:, b, :], in1=p['ab'][:, :], op=ALU.mult), reads=[delta, p['ab']], writes=[da], n=16)
                for b in range(NB):
                    bs = slice(b * 128, (b + 1) * 128)
                    f.skip = stage < 5.3
                    R = Fr()
                    f.op('dve', lambda R=R, b=b: V.tensor_tensor(out=v4(R), in0=lmask[:, :].unsqueeze(1).to_broadcast([128, 4, 128]),
                                                                 in1=da[:, b, :].unsqueeze(2).to_broadcast([128, 4, 128]), op=ALU.mult),
                         reads=[lmask, da], writes=[R], n=512)
                    bseg = bank('ssd')
                    f.op('pe', lambda bseg=bseg, R=R: P_.matmul(bseg[:, :], lhsT=umask[:, :], rhs=R[:, 0:512], start=True, stop=False), reads=[umask, R], writes=[bseg], n=2048)
                    f.op('pe', lambda bseg=bseg: P_.matmul(bseg[:, :], lhsT=ident_b[:, :], rhs=negm[:, :], start=False, stop=True),
                         reads=[ident_b, negm], writes=[bseg], n=512)
                    decay = Fr()
                    f.op('act', lambda bseg=bseg, decay=decay: S_.activation(out=decay[:, 0:512], in_=bseg[:, :], func=AF.Exp), reads=[bseg], writes=[decay])
                    bacs = bank('ssd')
                    f.op('pe', lambda bacs=bacs, R=R: P_.matmul(bacs[:, :], lhsT=ones_f[:, :], rhs=R[:, 0:512], start=True, stop=True), reads=[ones_f, R], writes=[bacs], n=2048)
                    eacs = Fr()
                    f.op('act', lambda bacs=bacs, eacs=eacs: S_.activation(out=eacs[:, 0:512], in_=bacs[:, :], func=AF.Exp), reads=[bacs], writes=[eacs])
                    bsm = bank('ssd')
                    f.op('pe', lambda bsm=bsm, b=b: P_.matmul(bsm[:, 0:4], lhsT=umask[:, :], rhs=da[:, b, :], start=True, stop=True), reads=[umask, da], writes=[bsm], n=64)
                    f.op('pe', lambda bsm=bsm, b=b: P_.matmul(bsm[:, 4:8], lhsT=ones_f[:, :], rhs=da[:, b, :], start=True, stop=True), reads=[ones_f, da], writes=[bsm], n=64)
                    tecd = ring('tecd', [128, 8], F32, 2)
                    f.op('act', lambda bsm=bsm, tecd=tecd: S_.activation(out=tecd[:, :], in_=bsm[:, 0:8], func=AF.Exp), reads=[bsm], writes=[tecd], n=16)
                    dte = ring('dte', [128, 4], F32, 2)
                    f.op('dve', lambda tecd=tecd, dte=dte, b=b: V.tensor_tensor(out=dte[:, :], in0=tecd[:, 0:4], in1=delta[:, b, :], op=ALU.mult), reads=[tecd, delta], writes=[dte], n=16)
                    f.skip = stage < 5.4
                    bankb = bank('ssd')
                    for q in range(4):
                        f.op('pe', lambda q=q, bs=bs, bankb=bankb: P_.matmul(bankb[:, q * 128:(q + 1) * 128], lhsT=xbc[:, q, bs], rhs=ident_b[:, :], start=True, stop=True), reads=[xk[q], ident_b], writes=[bankb], n=128)
                    xdt = Br(); xdte = Br(); btok = Br()
                    f.op('dve', lambda xdt=xdt, b=b, bankb=bankb: V.tensor_tensor(out=xdt[:, 0:256].rearrange("p (h d) -> p h d", h=4), in0=bankb[:, 0:256].rearrange("p (h d) -> p h d", h=4),
                                                                                in1=delta[:, b, :].unsqueeze(2).to_broadcast([128, 4, 64]), op=ALU.mult), reads=[bankb, delta], writes=[xdt], n=256)
                    f.op('dve', lambda xdte=xdte, dte=dte, bankb=bankb: V.tensor_tensor(out=xdte[:, 0:256].rearrange("p (h d) -> p h d", h=4), in0=bankb[:, 0:256].rearrange("p (h d) -> p h d", h=4),
                                                                                      in1=dte[:, :].unsqueeze(2).to_broadcast([128, 4, 64]), op=ALU.mult), reads=[bankb, dte], writes=[xdte], n=256)
                    f.op('dve', lambda btok=btok, bankb=bankb: V.tensor_copy(out=btok[:, 0:256], in_=bankb[:, 256:512]), reads=[bankb], writes=[btok], n=256)
                    f.skip = stage < 5.5
                    bsc = bank('ssd')
                    for g in range(2):
                        f.op('pe', lambda g=g, bsc=bsc, bs=bs: P_.matmul(bsc[:, g * 128:(g + 1) * 128], lhsT=xbc[:, 2 + g, bs], rhs=xbc[:, 4 + g, bs], start=True, stop=True),
                             reads=[xk[2 + g], xk[4 + g]], writes=[bsc], n=128)
                    scT = Br(); cp = Br()
                    for g in range(2):
                        f.op('dve', lambda g=g, bsc=bsc, scT=scT, decay=decay: V.tensor_tensor(out=scT[:, g * 256:(g + 1) * 256].rearrange("p (h l) -> p h l", h=2),
                                                                                             in0=bsc[:, g * 128:(g + 1) * 128].unsqueeze(1).to_broadcast([128, 2, 128]),
                                                                                             in1=decay[:, g * 256:(g + 1) * 256].rearrange("p (h l) -> p h l", h=2), op=ALU.mult),
                             reads=[bsc, decay], writes=[scT], n=256)
                        f.op('pool', lambda g=g, cp=cp, eacs=eacs, bs=bs: G.tensor_tensor(out=cp[:, g * 256:(g + 1) * 256].rearrange("p (h l) -> p h l", h=2),
                                                                                        in0=xbc[:, 4 + g, bs].unsqueeze(1).to_broadcast([128, 2, 128]),
                                                                                        in1=eacs[:, g * 256:(g + 1) * 256].rearrange("p (h l) -> p h l", h=2), op=ALU.mult),
                             reads=[xk[4 + g], eacs], writes=[cp], n=256)
                    f.skip = stage < 5.6
                    by = bank('ssd')
                    for hd in range(4):
                        jj, hf = hd // 2, hd % 2
                        osl = by[hf * 64:hf * 64 + 64, jj * 128:(jj + 1) * 128]
                        f.op('pe', lambda hd=hd, osl=osl, xdt=xdt, scT=scT: P_.matmul(osl, lhsT=xdt[:, hd * 64:(hd + 1) * 64], rhs=scT[:, hd * 128:(hd + 1) * 128], start=True, stop=False),
                             reads=[xdt, scT], writes=[by], n=128)
                        f.op('pe', lambda hd=hd, osl=osl, cp=cp: P_.matmul(osl, lhsT=p['prev_b'][:, hd * 64:(hd + 1) * 64], rhs=cp[:, hd * 128:(hd + 1) * 128], start=False, stop=True),
                             reads=[p['prev_b'], cp], writes=[by], n=128)
                    f.skip = stage < 5.7
                    bst = bank('ssd')
                    for g in range(2):
                        f.op('pe', lambda g=g, bst=bst, btok=btok, xdte=xdte: P_.matmul(bst[:, g * 128:(g + 1) * 128], lhsT=btok[:, g * 128:(g + 1) * 128], rhs=xdte[:, g * 128:(g + 1) * 128],
                                                                                      start=True, stop=True), reads=[btok, xdte], writes=[bst], n=128)
                    f.op('dve', lambda tecd=tecd: V.tensor_tensor(out=p['prev_f'][:, :].rearrange("p (h d) -> p h d", h=4), in0=p['prev_f'][:, :].rearrange("p (h d) -> p h d", h=4),
                                                                 in1=tecd[:, 4:8].unsqueeze(2).to_broadcast([128, 4, 64]), op=ALU.mult), reads=[p['prev_f'], tecd], writes=[p['prev_f']], n=256)
                    f.op('dve', lambda bst=bst: V.tensor_tensor(out=p['prev_f'][:, :], in0=p['prev_f'][:, :], in1=bst[:, 0:256], op=ALU.add), reads=[p['prev_f'], bst], writes=[p['prev_f']], n=256)
                    f.op('act', lambda: S_.copy(out=p['prev_b'][:, :], in_=p['prev_f'][:, :]), reads=[p['prev_f']], writes=[p['prev_b']])
                    for j in range(2):
                        f.op('dve', lambda j=j, by=by, bs=bs: V.scalar_tensor_tensor(out=ytile[:, j, bs], in0=xbc[:, j, bs], scalar=p['dsk'][:, j:j + 1], in1=by[:, j * 128:(j + 1) * 128],
                                                                                   op0=ALU.mult, op1=ALU.add), reads=[xk[j], p['dsk'], by], writes=[yk[j]], n=16)
                for j in range(2):
                    f.op('dve', lambda j=j: V.tensor_tensor(out=ytile[:, j, :], in0=ytile[:, j, :], in1=zs[:, j, :], op=ALU.mult), reads=[yk[j], zk[j]], writes=[yk[j]])
                    sq = Br()
                    f.op('act', lambda j=j, sq=sq: S_.activation(out=sq[:, :], in_=ytile[:, j, :], func=AF.Square), reads=[yk[j]], writes=[sq])
                    bk = bank('misc')
                    f.op('pe', lambda bk=bk, sq=sq: P_.matmul(bk[:, :], lhsT=ones_b[:, :], rhs=sq[:, :], start=True, stop=True), reads=[ones_b, sq], writes=[bk])
                    rs = Fr()
                    f.op('act', lambda bk=bk, rs=rs: S_.activation(out=rs[:, 0:T], in_=bk[:, :], func=AF.Ln, scale=1.0 / 128, bias=EPS), reads=[bk], writes=[rs])
                    f.op('act', lambda rs=rs: S_.activation(out=rs[:, 0:T], in_=rs[:, 0:T], func=AF.Exp, scale=-0.5), reads=[rs], writes=[rs])
                    f.op('dve', lambda j=j, rs=rs: V.scalar_tensor_tensor(out=mx[:, 6 + j, :], in0=ytile[:, j, :], scalar=p['ng'][:, j:j + 1], in1=rs[:, 0:T], op0=ALU.mult, op1=ALU.mult),
                         reads=[yk[j], p['ng'], rs], writes=[mxk[6 + j]])
                f.skip = False
                if dbg and ps_i == 0 and l == 0 and t == 0:
                    for k in range(KC):
                        mxf = Fr()
                        f.op('dve', lambda k=k, mxf=mxf: V.tensor_copy(out=mxf[:, 0:T], in_=mx[:, k, :]), reads=[mxk[k]], writes=[mxf])
                        dump('mix', mxf[:, 0:T], mxf) if False else None
                        if 'mix' in dbg_aps:
                            f.dma('sp', dbg_aps['mix'][:, k, :], mxf[:, 0:T], reads=[mxf], writes=[dbgb])

                f.skip = stage < 6
                for m in range(KC):
                    bk = bank('misc')
                    for k in range(KC):
                        f.op('pe', lambda k=k, m=m, bk=bk: P_.matmul(bk[:, :], lhsT=WO[:, k, m * 128:(m + 1) * 128], rhs=mx[:, k, :], start=(k == 0), stop=(k == KC - 1)),
                             reads=[sWO, mxk[k]], writes=[bk])
                    f.op('dve', lambda m=m, bk=bk: V.tensor_tensor(out=h[:, m, :], in0=h[:, m, :], in1=bk[:, :], op=ALU.add), reads=[hk[t][m], bk], writes=[hk[t][m]])
                rmsnorm_tile(h, hn2[t], p['g2'], hn2k[t])
            item_idx += 3
            ensure_loaded(item_idx + NSLOT - 1)

            f.skip = stage < 7
            for gi in range(len(FGROUPS)):
                it = loaded[item_idx]
                slot = it['slot']
                Gc = it['G']
                for t in range(NT):
                    h = hT[t]
                    ffact = ffacts[t % 2]
                    for j in range(Gc):
                        bg = bank('gu'); bu = bank('gu')
                        for k in range(KC):
                            f.op('pe', lambda k=k, j=j, bg=bg: P_.matmul(bg[:, :], lhsT=it['wg'][:, k, j * 128:(j + 1) * 128], rhs=hn2[t][:, k, :], start=(k == 0), stop=(k == KC - 1)),
                                 reads=[slot, hn2k[t][k]], writes=[bg])
                        for k in range(KC):
                            f.op('pe', lambda k=k, j=j, bu=bu: P_.matmul(bu[:, :], lhsT=it['wu'][:, k, j * 128:(j + 1) * 128], rhs=hn2[t][:, k, :], start=(k == 0), stop=(k == KC - 1)),
                                 reads=[slot, hn2k[t][k]], writes=[bu])
                        sg = Fr()
                        f.op('act', lambda bg=bg, sg=sg: S_.activation(out=sg[:, 0:T], in_=bg[:, :], func=AF.Silu), reads=[bg], writes=[sg])
                        f.op('dve', lambda j=j, bu=bu, sg=sg: V.tensor_tensor(out=ffact[:, j, :], in0=bu[:, :], in1=sg[:, 0:T], op=ALU.mult), reads=[bu, sg], writes=[fk[t % 2][j]])
                    for m in range(KC):
                        bk = bank('dn')
                        for j in range(Gc):
                            f.op('pe', lambda j=j, m=m, bk=bk: P_.matmul(bk[:, :], lhsT=it['wd'][:, j, m * 128:(m + 1) * 128], rhs=ffact[:, j, :], start=(j == 0), stop=(j == Gc - 1)),
                                 reads=[slot, fk[t % 2][j]], writes=[bk])
                        f.op('dve', lambda m=m, bk=bk, h=h: V.tensor_tensor(out=h[:, m, :], in0=h[:, m, :], in1=bk[:, :], op=ALU.add), reads=[hk[t][m], bk], writes=[hk[t][m]])
                item_idx += 1
                ensure_loaded(item_idx + NSLOT - 1)
            if dbg and ps_i == 0 and l == 0 and 'h0' in dbg_aps:
                f.dma('sp', dbg_aps['h0'], hT[0][:, :, :], reads=hk[0], writes=[dbgb])

        f.skip = False
        for t in range(NT):
            h = hT[t]
            if final:
                acc = bank('misc')
                for k in range(KC):
                    sq = Br()
                    f.op('act', lambda k=k, sq=sq: S_.activation(out=sq[:, :], in_=h[:, k, :], func=AF.Square), reads=[hk[t][k]], writes=[sq])
                    f.op('pe', lambda k=k, sq=sq, acc=acc: P_.matmul(acc[:, :], lhsT=ones_b[:, :], rhs=sq[:, :], start=(k == 0), stop=(k == KC - 1)), reads=[ones_b, sq], writes=[acc])
                rs = Fr()
                f.op('act', lambda acc=acc, rs=rs: S_.activation(out=rs[:, 0:T], in_=acc[:, :], func=AF.Ln, scale=1.0 / D, bias=EPS), reads=[acc], writes=[rs])
                f.op('act', lambda rs=rs: S_.activation(out=rs[:, 0:T], in_=rs[:, 0:T], func=AF.Exp, scale=-0.5), reads=[rs], writes=[rs])
                for k in range(KC):
                    f.op('dve', lambda k=k, rs=rs: V.scalar_tensor_tensor(out=h[:, k, :], in0=h[:, k, :], scalar=gfin[:, k:k + 1], in1=rs[:, 0:T], op0=ALU.mult, op1=ALU.mult),
                         reads=[hk[t][k], gfin, rs], writes=[hk[t][k]])
            for b in range(NB):
                r0 = tok0 + t * T + b * 128
                for half in range(2):
                    bk = bank('misc')
                    for kk in range(4):
                        k = half * 4 + kk
                        f.op('pe', lambda k=k, kk=kk, bk=bk, b=b: P_.transpose(bk[:, kk * 128:(kk + 1) * 128], h[:, k, b * 128:(b + 1) * 128], ident_f[:, :]),
                             reads=[hk[t][k], ident_f], writes=[bk], n=512)
                    yo = Fr()
                    f.op('act', lambda bk=bk, yo=yo: S_.copy(out=yo[:, 0:512], in_=bk[:, :]), reads=[bk], writes=[yo])
                    f.dma('sp', out_d[s_loc, r0:r0 + 128, half * 512:(half + 1) * 512], yo[:, 0:512], reads=[yo], writes=[outb])
    f.finish()
    f.schedule()
    f.emit()
    print('sched makespan us', f.makespan / 1e3)
    print('build: instr counts', f.cnt, 'waits', f.nwait, 'sbuf left', nc.sbuf_bytes_remaining)
    return nc


_NC_CACHE = {}


def kernel(**inputs):
    n = 8
    x = np.ascontiguousarray(inputs['x'], dtype=np.float32)
    if 'full' not in _NC_CACHE:
        _NC_CACHE['full'] = build()
    nc = _NC_CACHE['full']
    in_maps = []
    for c in range(n):
        m = {'x': np.ascontiguousarray(x[2 * c:2 * c + 2])}
        for nm in PARAM_NAMES:
            m[nm] = np.ascontiguousarray(inputs[nm], dtype=np.float32)
        in_maps.append(m)
    res = run_bass_kernel_spmd(nc, in_maps, core_ids=list(range(n)))
    out = np.concatenate([np.asarray(r['out'], dtype=np.float32) for r in res.results], axis=0)
    return out
```
